# Optimizing a Trainium2 kernel written in Bass

```python
import jax, jax.numpy as jnp
from jax import lax
import numpy as np

D_MODEL = 1024
BATCH = 8
SEQ = 4096
DEPTH = 2

CTX_LEN = 256
GRID_W = 64
HEAD_DIM = 64
H_ML = 4
H_NA = 6
H_GQ = 6
H_KV = 2
D_ML = H_ML * HEAD_DIM
D_NA = H_NA * HEAD_DIM
D_GQ = H_GQ * HEAD_DIM
D_KV = H_KV * HEAD_DIM
GQ_GROUP = H_GQ // H_KV
N_BRANCH = 3
D_FF = 2816
NA_WIN_R = 8
NA_WIN_C = 16
ML_CHUNK = 64
Q_BLOCK = 128
ROPE_THETA = 10000.0
EPS = 1e-6
N_MOD = 9
ATTN_SCALE = HEAD_DIM ** -0.5
KV_SPLITS = (D_ML, D_ML, 4 * H_ML, D_NA, D_NA, D_KV, D_KV)
Q_SPLITS = (D_ML, D_ML, D_NA, D_GQ, N_BRANCH * D_MODEL)
N_KV_COLS = sum(KV_SPLITS)
N_IN = N_KV_COLS + sum(Q_SPLITS)

kernel_name = 'hybrid_mlstm_natten_gqa_prefix_block'


def _split(a, sizes):
    return jnp.split(a, np.cumsum(sizes)[:-1].tolist(), axis=-1)


def _heads(a, n):
    return a.reshape(a.shape[:-1] + (n, HEAD_DIM))


def rms_norm(x, w):
    x32 = x.astype(jnp.float32)
    y = x32 * lax.rsqrt(jnp.mean(x32 * x32, axis=-1, keepdims=True) + EPS)
    return (y * w.astype(jnp.float32)).astype(x.dtype)


def modulate(x, w_norm, shift, scale):
    return rms_norm(x, w_norm) * (1 + scale) + shift


def swiglu(h, w_in, w_out):
    g, u = jnp.split(h @ w_in, 2, axis=-1)
    return (jax.nn.silu(g) * u) @ w_out


def axial_rope(n_tok):
    t = jnp.arange(n_tok, dtype=jnp.int32)
    row = (t // GRID_W).astype(jnp.float32)
    col = (t % GRID_W).astype(jnp.float32)
    n_freq = HEAD_DIM // 4
    inv = ROPE_THETA ** (-jnp.arange(n_freq, dtype=jnp.float32) / n_freq)
    ang = jnp.concatenate([row[:, None] * inv, col[:, None] * inv], axis=-1)
    return jnp.cos(ang)[:, None, :], jnp.sin(ang)[:, None, :]


def apply_rope(x, cos, sin):
    x32 = x.astype(jnp.float32)
    x1, x2 = jnp.split(x32, 2, axis=-1)
    return jnp.concatenate([x1 * cos - x2 * sin, x1 * sin + x2 * cos], axis=-1).astype(x.dtype)


def attend(q, k, v):
    s = jnp.einsum('btkgd,bnkd->bkgtn', q, k).astype(jnp.float32) * ATTN_SCALE
    p = jax.nn.softmax(s, axis=-1).astype(v.dtype)
    return jnp.einsum('bkgtn,bnkd->btkgd', p, v)


def gqa_latent(q, k, v, kc, vc):
    B, S = q.shape[:2]
    k_all = jnp.concatenate([k, kc], axis=1)
    v_all = jnp.concatenate([v, vc], axis=1)
    qb = q.reshape(B, S // Q_BLOCK, Q_BLOCK, H_KV, GQ_GROUP, HEAD_DIM).swapaxes(0, 1)
    out = lax.map(lambda qi: attend(qi, k_all, v_all), qb)
    return out.swapaxes(0, 1).reshape(B, S, D_GQ)


def na_latent(q, k, v, kc, vc, rpb, n_rows):
    B, S, H, d = q.shape
    wr = min(NA_WIN_R, n_rows)
    n_win = wr * NA_WIN_C
    col = np.arange(GRID_W)
    col_idx = np.clip(col - NA_WIN_C // 2, 0, GRID_W - NA_WIN_C)[:, None] + np.arange(NA_WIN_C)[None, :]
    col_off = col_idx - col[:, None] + NA_WIN_C - 1
    grid = lambda a: a.reshape(B, n_rows, GRID_W, H, d)
    kg, vg = grid(k), grid(v)
    qg = jnp.swapaxes(grid(q), 0, 1)

    def row_block(args):
        r, q_r = args
        rs = jnp.clip(r - wr // 2, 0, n_rows - wr)

        def window(a):
            band = lax.dynamic_slice_in_dim(a, rs, wr, axis=1)
            return jnp.swapaxes(band[:, :, col_idx], 1, 2).reshape(B, GRID_W, n_win, H, d)

        kw, vw = window(kg), window(vg)
        row_off = rs + jnp.arange(wr) - r + NA_WIN_R - 1
        bias = rpb[:, row_off][:, :, col_off]
        bias = jnp.swapaxes(bias, 1, 2).reshape(H, GRID_W, n_win).astype(jnp.float32)
        s_win = jnp.einsum('bwhd,bwnhd->bhwn', q_r, kw).astype(jnp.float32) * ATTN_SCALE + bias
        s_ctx = jnp.einsum('bwhd,blhd->bhwl', q_r, kc).astype(jnp.float32) * ATTN_SCALE
        p = jax.nn.softmax(jnp.concatenate([s_win, s_ctx], axis=-1), axis=-1).astype(v.dtype)
        return (jnp.einsum('bhwn,bwnhd->bwhd', p[..., :n_win], vw)
                + jnp.einsum('bhwl,blhd->bwhd', p[..., n_win:], vc))

    out = lax.map(row_block, (jnp.arange(n_rows, dtype=jnp.int32), qg))
    return jnp.swapaxes(out, 0, 1).reshape(B, S, H * d)


def mlstm_scan(q, k, v, ig, lf, state):
    B, T, H, d = k.shape
    nc = T // ML_CHUNK
    chunked = lambda a: jnp.swapaxes(a.reshape((B, nc, ML_CHUNK) + a.shape[2:]), 0, 1)
    with_out = q is not None
    xs = (chunked(k), chunked(v), chunked(ig), chunked(lf)) + ((chunked(q),) if with_out else ())
    tri = jnp.asarray(np.tril(np.ones((ML_CHUNK, ML_CHUNK), dtype=bool)))

    def step(carry, xs_c):
        C, n, m = carry
        kc, vc, ic, fc = xs_c[:4]
        b = jnp.cumsum(fc, axis=1)
        b_tot = b[:, -1]
        w_end = b_tot[:, None] - b + ic
        m_new = jnp.maximum(b_tot + m, jnp.max(w_end, axis=1))
        a_end = jnp.exp(w_end - m_new[:, None])
        decay = jnp.exp(b_tot + m - m_new)
        C_new = decay[..., None, None] * C + jnp.einsum('blh,blhk,blhv->bhkv', a_end, kc, vc)
        n_new = decay[..., None] * n + jnp.einsum('blh,blhk->bhk', a_end, kc)
        if not with_out:
            return (C_new, n_new, m_new), None
        q_c = xs_c[4]
        m_inter = b + m[:, None]
        logw = b[:, :, None] - b[:, None, :] + ic[:, None]
        logw = jnp.where(tri[None, :, :, None], logw, -jnp.inf)
        m_j = jnp.maximum(m_inter, jnp.max(logw, axis=2))
        w = jnp.exp(logw - m_j[:, :, None])
        g = jnp.exp(m_inter - m_j)
        qk = jnp.einsum('bjhd,bshd->bjsh', q_c, kc) * w
        num = jnp.einsum('bjsh,bshd->bjhd', qk, vc) + g[..., None] * jnp.einsum('bjhk,bhkv->bjhv', q_c, C)
        den = jnp.sum(qk, axis=2) + g * jnp.einsum('bjhk,bhk->bjh', q_c, n)
        h = num / jnp.maximum(jnp.abs(den), jnp.exp(-m_j))[..., None]
        return (C_new, n_new, m_new), h

    state, hs = lax.scan(step, state, xs)
    h = None if hs is None else jnp.swapaxes(hs, 0, 1).reshape(B, T, H, d)
    return state, h


def mlstm_gates(pre, bias):
    g = (pre.astype(jnp.float32) + bias.astype(jnp.float32)).reshape(pre.shape[:-1] + (4, H_ML))
    fwd = (g[..., 0, :], jax.nn.log_sigmoid(g[..., 1, :]))
    bwd = (g[..., 2, :], jax.nn.log_sigmoid(g[..., 3, :]))
    return fwd, bwd


def _flip(a):
    return None if a is None else jnp.flip(a, axis=1)


def mlstm_bidir(q, k, v, g_pre, qc, kc, vc, gc_pre, gate_b):
    f32 = jnp.float32
    B = k.shape[0]
    q, k, v = q.astype(f32), k.astype(f32) * ATTN_SCALE, v.astype(f32)
    kc, vc = kc.astype(f32) * ATTN_SCALE, vc.astype(f32)
    qc = None if qc is None else qc.astype(f32)
    (ig_f, lf_f), (ig_b, lf_b) = mlstm_gates(g_pre, gate_b)
    (igc_f, lfc_f), (igc_b, lfc_b) = mlstm_gates(gc_pre, gate_b)
    init = (jnp.zeros((B, H_ML, HEAD_DIM, HEAD_DIM), f32), jnp.zeros((B, H_ML, HEAD_DIM), f32),
            jnp.zeros((B, H_ML), f32))
    st_f, hc_f = mlstm_scan(qc, kc, vc, igc_f, lfc_f, init)
    _, h_f = mlstm_scan(q, k, v, ig_f, lf_f, st_f)
    st_b, hc_b = mlstm_scan(_flip(qc), _flip(kc), _flip(vc), _flip(igc_b), _flip(lfc_b), init)
    _, h_b = mlstm_scan(_flip(q), _flip(k), _flip(v), _flip(ig_b), _flip(lf_b), st_b)
    h = h_f + _flip(h_b)
    h_c = None if qc is None else hc_f + _flip(hc_b)
    return h, h_c


def mlstm_out(h, o, norm_w):
    y = rms_norm(h, norm_w.reshape(H_ML, HEAD_DIM)).astype(o.dtype)
    return y.reshape(o.shape) * jax.nn.sigmoid(o)


def merge_branches(br_g, o_ml, o_na, o_gq, w_br_ml, w_br_na, w_br_gq, w_out):
    g_ml, g_na, g_gq = jnp.split(jax.nn.sigmoid(br_g), N_BRANCH, axis=-1)
    y = g_ml * (o_ml @ w_br_ml) + g_na * (o_na @ w_br_na) + g_gq * (o_gq @ w_br_gq)
    return y @ w_out


def token_mixer(hx, hc, n_rows, cos, sin, w_in, ml_gate_b, ml_norm_w, na_qk_w, na_rpb, gq_qk_w,
                w_br_ml, w_br_na, w_br_gq, w_out, ctx_out):
    B, S, _ = hx.shape
    L = hc.shape[1]
    px = hx @ w_in
    pc = hc @ (w_in if ctx_out else w_in[:, :N_KV_COLS])
    ml_k, ml_v, ml_g, na_k, na_v, gq_k, gq_v = _split(px[..., :N_KV_COLS], KV_SPLITS)
    ml_q, ml_o, na_q, gq_q, br_g = _split(px[..., N_KV_COLS:], Q_SPLITS)
    cml_k, cml_v, cml_g, cna_k, cna_v, cgq_k, cgq_v = _split(pc[..., :N_KV_COLS], KV_SPLITS)
    if ctx_out:
        cml_q, cml_o, cna_q, cgq_q, cbr_g = _split(pc[..., N_KV_COLS:], Q_SPLITS)

    h_ml, hc_ml = mlstm_bidir(_heads(ml_q, H_ML), _heads(ml_k, H_ML), _heads(ml_v, H_ML), ml_g,
                              _heads(cml_q, H_ML) if ctx_out else None,
                              _heads(cml_k, H_ML), _heads(cml_v, H_ML), cml_g, ml_gate_b)
    o_ml = mlstm_out(h_ml, ml_o, ml_norm_w)

    nq = rms_norm(_heads(na_q, H_NA), na_qk_w[0])
    nk = rms_norm(_heads(na_k, H_NA), na_qk_w[1])
    cnk = rms_norm(_heads(cna_k, H_NA), na_qk_w[1])
    cnv = _heads(cna_v, H_NA)
    o_na = na_latent(nq, nk, _heads(na_v, H_NA), cnk, cnv, na_rpb, n_rows)

    gq = apply_rope(rms_norm(_heads(gq_q, H_GQ), gq_qk_w[0]), cos, sin)
    gk = apply_rope(rms_norm(_heads(gq_k, H_KV), gq_qk_w[1]), cos, sin)
    cgk = rms_norm(_heads(cgq_k, H_KV), gq_qk_w[1])
    cgv = _heads(cgq_v, H_KV)
    o_gq = gqa_latent(gq, gk, _heads(gq_v, H_KV), cgk, cgv)

    yx = merge_branches(br_g, o_ml, o_na, o_gq, w_br_ml, w_br_na, w_br_gq, w_out)
    if not ctx_out:
        return yx, None
    co_ml = mlstm_out(hc_ml, cml_o, ml_norm_w)
    cq_na = rms_norm(_heads(cna_q, H_NA), na_qk_w[0])[:, :, :, None, :]
    co_na = attend(cq_na, cnk, cnv).reshape(B, L, D_NA)
    cq_gq = rms_norm(_heads(cgq_q, H_GQ), gq_qk_w[0]).reshape(B, L, H_KV, GQ_GROUP, HEAD_DIM)
    co_gq = attend(cq_gq, cgk, cgv).reshape(B, L, D_GQ)
    yc = merge_branches(cbr_g, co_ml, co_na, co_gq, w_br_ml, w_br_na, w_br_gq, w_out)
    return yx, yc


def setup_inputs(seed: int = 0) -> dict:
    key = jax.random.key(seed)
    ks = jax.random.split(key, 20)
    nrm = lambda k, shape, s: s * jax.random.normal(k, shape, jnp.float32)
    gate_base = np.concatenate([np.zeros(H_ML), np.linspace(3.0, 6.0, H_ML),
                                np.zeros(H_ML), np.linspace(3.0, 6.0, H_ML)]).astype(np.float32)
    return {
        'x': nrm(ks[0], (BATCH, SEQ, D_MODEL), 1.0),
        'c': nrm(ks[1], (BATCH, D_MODEL), 1.0),
        'ctx': nrm(ks[2], (BATCH, CTX_LEN, D_MODEL), 1.0),
        'c_ctx': nrm(ks[3], (D_MODEL,), 1.0),
        'ada_w': nrm(ks[4], (DEPTH, D_MODEL, N_MOD * D_MODEL), 0.5 * D_MODEL ** -0.5),
        'ada_b': nrm(ks[5], (DEPTH, N_MOD * D_MODEL), 0.01),
        'norm_w': 1.0 + nrm(ks[6], (DEPTH, 3, D_MODEL), 0.02),
        'ffn_w_in': nrm(ks[7], (DEPTH, 2, D_MODEL, 2 * D_FF), D_MODEL ** -0.5),
        'ffn_w_out': nrm(ks[8], (DEPTH, 2, D_FF, D_MODEL), D_FF ** -0.5),
        'mix_w_in': nrm(ks[9], (DEPTH, D_MODEL, N_IN), D_MODEL ** -0.5),
        'ml_gate_b': jnp.asarray(gate_base)[None, :] + nrm(ks[10], (DEPTH, 4 * H_ML), 0.01),
        'ml_norm_w': 1.0 + nrm(ks[11], (DEPTH, D_ML), 0.02),
        'na_qk_w': 1.0 + nrm(ks[12], (DEPTH, 2, HEAD_DIM), 0.02),
        'na_rpb': nrm(ks[13], (DEPTH, H_NA, 2 * NA_WIN_R - 1, 2 * NA_WIN_C - 1), 0.2),
        'gq_qk_w': 1.0 + nrm(ks[14], (DEPTH, 2, HEAD_DIM), 0.02),
        'w_br_ml': nrm(ks[15], (DEPTH, D_ML, D_MODEL), D_ML ** -0.5),
        'w_br_na': nrm(ks[16], (DEPTH, D_NA, D_MODEL), D_NA ** -0.5),
        'w_br_gq': nrm(ks[17], (DEPTH, D_GQ, D_MODEL), D_GQ ** -0.5),
        'w_out': nrm(ks[18], (DEPTH, D_MODEL, D_MODEL), D_MODEL ** -0.5),
    }


def reference(x, c, ctx, c_ctx, ada_w, ada_b, norm_w, ffn_w_in, ffn_w_out, mix_w_in, ml_gate_b,
              ml_norm_w, na_qk_w, na_rpb, gq_qk_w, w_br_ml, w_br_na, w_br_gq, w_out):
    n_rows = x.shape[1] // GRID_W
    cos, sin = axial_rope(x.shape[1])
    xc = ctx
    silu_c = jax.nn.silu(c)
    silu_cc = jax.nn.silu(c_ctx)
    for l in range(DEPTH):
        ctx_out = l < DEPTH - 1
        mx = jnp.split((silu_c @ ada_w[l] + ada_b[l])[:, None, :], N_MOD, axis=-1)
        mc = jnp.split(silu_cc @ ada_w[l] + ada_b[l], N_MOD, axis=-1)
        x = x + 0.5 * mx[2] * swiglu(modulate(x, norm_w[l, 0], mx[0], mx[1]), ffn_w_in[l, 0], ffn_w_out[l, 0])
        xc = xc + 0.5 * mc[2] * swiglu(modulate(xc, norm_w[l, 0], mc[0], mc[1]), ffn_w_in[l, 0], ffn_w_out[l, 0])
        yx, yc = token_mixer(modulate(x, norm_w[l, 1], mx[3], mx[4]), modulate(xc, norm_w[l, 1], mc[3], mc[4]),
                             n_rows, cos, sin, mix_w_in[l], ml_gate_b[l], ml_norm_w[l], na_qk_w[l], na_rpb[l],
                             gq_qk_w[l], w_br_ml[l], w_br_na[l], w_br_gq[l], w_out[l], ctx_out)
        x = x + mx[5] * yx
        x = x + 0.5 * mx[8] * swiglu(modulate(x, norm_w[l, 2], mx[6], mx[7]), ffn_w_in[l, 1], ffn_w_out[l, 1])
        if ctx_out:
            xc = xc + mc[5] * yc
            xc = xc + 0.5 * mc[8] * swiglu(modulate(xc, norm_w[l, 2], mc[6], mc[7]), ffn_w_in[l, 1], ffn_w_out[l, 1])
    return x
```

```python
import numpy as np
from contextlib import ExitStack
import concourse.bass as bass
import concourse.mybir as mybir
from concourse.bass_utils import run_bass_kernel_spmd

F32, BF16 = mybir.dt.float32, mybir.dt.bfloat16
AF = mybir.ActivationFunctionType
ALU = mybir.AluOpType

D = 1024
L_CTX = 256
DFF = 2816
NIN = 5904
EPS = 1e-6
NEG = -30000.0


class Tok:
    __slots__ = ("w", "r")

    def __init__(self):
        self.w = None
        self.r = []


class Op:
    __slots__ = ("eng", "fn", "deps", "sig", "sem", "key", "val", "dma")


class Sched:
    ENG = ("pe", "act", "dve", "pool", "sp")

    def __init__(self, nc, es, ndma=16):
        self.nc = nc
        self.e = {"pe": nc.tensor, "act": nc.scalar, "dve": nc.vector, "pool": nc.gpsimd, "sp": nc.sync}
        self.sem = {k: es.enter_context(nc.semaphore("s_" + k)) for k in self.ENG}
        self.cnt = {k: 0 for k in self.ENG}
        self.bars = [es.enter_context(nc.semaphore(f"s_bar{i}")) for i in range(10)]
        self.barcnt = 0
        self.dq = ("sp", "pool", "act")
        self.dsem = {q: [es.enter_context(nc.semaphore(f"d_{q}{i}")) for i in range(ndma)] for q in self.dq}
        self.dval = {q: [0] * ndma for q in self.dq}
        self.dnext = {q: 0 for q in self.dq}
        self.known = {k: {} for k in self.ENG}
        self.ops = []
        self.toks = []
        self.nins = 0
        self.nsig = {k: 0 for k in self.ENG}
        self.nflush = 0

    def tok(self):
        t = Tok()
        self.toks.append(t)
        return t

    def add(self, eng, fn, reads=(), writes=(), dma=False):
        op = Op()
        op.eng, op.fn, op.dma, op.sig, op.sem, op.key, op.val = eng, fn, dma, dma, None, None, 0
        deps = []
        for t in reads:
            if t.w is not None:
                deps.append(t.w)
        for t in writes:
            if t.w is not None:
                deps.append(t.w)
            deps.extend(t.r)
        seen = set()
        d2 = []
        for d in deps:
            if id(d) in seen:
                continue
            seen.add(id(d))
            if d.eng == "pe" and eng == "pe" and not d.dma and not dma:
                continue
            if not d.sig:
                d.sig = True
                self.nsig[d.eng] += 1
            d2.append(d)
        op.deps = d2
        for t in reads:
            if not dma:
                t.r = [o for o in t.r if o.dma or o.eng != eng]
            t.r.append(op)
        for t in writes:
            t.w = op
            t.r = []
        self.ops.append(op)
        return op

    def mm(self, out, lhsT, rhs, start, stop, reads=(), writes=()):
        return self.add("pe", lambda e: e.matmul(out, lhsT, rhs, start=start, stop=stop), reads, writes)

    def act(self, out, in_, func, reads=(), writes=(), **kw):
        return self.add("act", lambda e: e.activation(out=out, in_=in_, func=func, **kw), reads, writes)

    def tt(self, eng, out, in0, in1, op, reads=(), writes=()):
        return self.add(eng, lambda e: e.tensor_tensor(out=out, in0=in0, in1=in1, op=op), reads, writes)

    def ts(self, eng, out, in0, s1, s2, op0, op1=None, reads=(), writes=()):
        if op1 is None:
            return self.add(eng, lambda e: e.tensor_scalar(out=out, in0=in0, scalar1=s1, scalar2=None, op0=op0), reads, writes)
        return self.add(eng, lambda e: e.tensor_scalar(out=out, in0=in0, scalar1=s1, scalar2=s2, op0=op0, op1=op1), reads, writes)

    def stt(self, eng, out, in0, scalar, in1, op0, op1, reads=(), writes=()):
        return self.add(eng, lambda e: e.scalar_tensor_tensor(out=out, in0=in0, scalar=scalar, in1=in1, op0=op0, op1=op1), reads, writes)

    def copy(self, eng, out, in_, reads=(), writes=()):
        if eng == "act":
            return self.add("act", lambda e: e.activation(out=out, in_=in_, func=AF.Copy), reads, writes)
        return self.add(eng, lambda e: e.tensor_copy(out=out, in_=in_), reads, writes)

    def recip(self, out, in_, reads=(), writes=()):
        return self.add("dve", lambda e: e.reciprocal(out=out, in_=in_), reads, writes)

    def memset(self, eng, ap, val, writes=()):
        return self.add(eng, lambda e: e.memset(ap, val), (), writes)

    def dma(self, q, out, in_, reads=(), writes=()):
        return self.add(q, lambda e: e.dma_start(out=out, in_=in_), reads, writes, dma=True)

    def _wait(self, eng, sem, key, val):
        kn = self.known[eng]
        if kn.get(key, 0) < val:
            self.e[eng].wait_ge(sem, val)
            kn[key] = val
            self.nins += 1

    def flush(self):
        last = {}
        for op in self.ops:
            if not op.dma:
                last[op.eng] = op
        for op in last.values():
            op.sig = True
        for op in self.ops:
            E = self.e[op.eng]
            for d in op.deps:
                self._wait(op.eng, d.sem, d.key, d.val)
            if op.dma:
                q = op.eng
                i = self.dnext[q]
                self.dnext[q] = (i + 1) % len(self.dsem[q])
                s = self.dsem[q][i]
                key = f"d_{q}{i}"
                pv = self.dval[q][i]
                if pv > 0:
                    self._wait(q, s, key, pv)
                ins = op.fn(E)
                ins.then_inc(s, 16)
                self.dval[q][i] = pv + 16
                op.sem, op.key, op.val = s, key, pv + 16
            else:
                ins = op.fn(E)
                if op.sig:
                    self.cnt[op.eng] += 1
                    op.sem, op.key, op.val = self.sem[op.eng], "s_" + op.eng, self.cnt[op.eng]
                    ins.then_inc(op.sem, 1)
            self.nins += 1
        for k in self.ENG:
            if k != "sp" and self.cnt[k] > 0:
                self._wait("sp", self.sem[k], "s_" + k, self.cnt[k])
        if self.cnt["sp"] > 0:
            self._wait("sp", self.sem["sp"], "s_sp", self.cnt["sp"])
        for q in self.dq:
            for i, s in enumerate(self.dsem[q]):
                if self.dval[q][i] > 0:
                    self._wait("sp", s, f"d_{q}{i}", self.dval[q][i])
        bsem = self.bars[self.barcnt // 10]
        bval = self.barcnt % 10 + 1
        self.barcnt += 1
        self.e["sp"].sem_inc(bsem, 1)
        for k in self.ENG:
            if k != "sp":
                self.e[k].wait_ge(bsem, bval)
        for k in self.ENG:
            kn = self.known[k]
            for k2 in self.ENG:
                kn["s_" + k2] = self.cnt[k2]
            for q in self.dq:
                for i in range(len(self.dsem[q])):
                    kn[f"d_{q}{i}"] = self.dval[q][i]
        self.nsig = {k: 0 for k in self.ENG}
        self.nflush += 1
        for t in self.toks:
            t.w = None
            t.r = []
        self.ops = []


def _maybe_flush(self, limit=100000):
    if max(self.nsig.values()) > limit:
        self.flush()


Sched.maybe_flush = _maybe_flush


class Ring:
    def __init__(self, K, name, n, shape, dtype, psum=False):
        self.bufs = []
        for i in range(n):
            if psum:
                t = K.ps(name, shape, dtype)
            else:
                t = K.sb(name, shape, dtype)
            self.bufs.append((t, K.S.tok()))
        self.i = 0

    def next(self):
        b = self.bufs[self.i]
        self.i = (self.i + 1) % len(self.bufs)
        return b


class K:
    def __init__(self, S=4096, layers=2, debug=False, stop_after=None):
        self.S_lat = S
        self.T = L_CTX + S
        self.NR = S // 64
        self.NB = self.NR // 8
        self.NCH = self.T // 128
        self.layers = layers
        self.debug = debug
        self.stop_after = stop_after
        self.nc = bass.Bass("TRN2", target_bir_lowering=False)
        self.dbg_names = []

    def din(self, name, shape, dt=F32):
        return self.nc.dram_tensor(name, list(shape), dt, kind="ExternalInput").ap()

    def dscr(self, name, shape, dt=F32):
        if self.debug:
            self.dbg_names.append(name)
            return self.nc.dram_tensor(name, list(shape), dt, kind="ExternalOutput").ap()
        return self.nc.dram_tensor(name, list(shape), dt).ap()

    def uq(self, name):
        self._uq = getattr(self, "_uq", 0) + 1
        return f"{name}_{self._uq}"

    def sb(self, name, shape, dt=F32):
        return self.es.enter_context(self.nc.sbuf_tensor(self.uq(name), list(shape), dt))

    def ps(self, name, shape, dt=F32):
        return self.es.enter_context(self.nc.psum_tensor(self.uq(name), list(shape), dt))

    def tiles(self, n=512):
        out = [(0, L_CTX, True)] if n >= L_CTX else [(i * n, n, True) for i in range(L_CTX // n)]
        out += [(L_CTX + n * i, n, False) for i in range(self.S_lat // n)]
        return out

    def done(self, name):
        return self.stop_after == name

    def load_w(self, dst, src, kc_n, ncols, stage, colblk):
        S = self.S
        srcv = src.rearrange("(kc p) n -> p kc n", p=128)
        engs = ("act", "dve", "pool", "dve", "act")
        i = 0
        for kc in range(kc_n):
            for c0 in range(0, ncols, colblk):
                cw = min(colblk, ncols - c0)
                st, stok = stage.next()
                S.dma(("sp", "pool", "act")[i % 3], st[:, 0:cw], srcv[:, kc, c0:c0 + cw], writes=[stok])
                S.copy(engs[i % 5], dst[:, kc, c0:c0 + cw], st[:, 0:cw], reads=[stok], writes=[self.wtok])
                i += 1

    def modulate(self, xt, xtok, n, j, c, h, htok, sq_ring, ps_ms, ps_ms_tok, sd, sdtok):
        S = self.S
        for kc in range(8):
            sq, sqtok = sq_ring.next()
            S.act(sq[:, 0:n], xt[:, kc, 0:n], AF.Square, reads=[xtok], writes=[sqtok])
            S.mm(ps_ms[:, 0:n], self.c_onesD, sq[:, 0:n], kc == 0, kc == 7, reads=[sqtok], writes=[ps_ms_tok])
        S.act(sd[:, 0:n], ps_ms[:, 0:n], AF.Sqrt, reads=[ps_ms_tok], writes=[sdtok], bias=self.c_eps, scale=1.0)
        S.recip(sd[:, 0:n], sd[:, 0:n], reads=[sdtok], writes=[sdtok])
        for kc in range(8):
            tmp, ttok = sq_ring.next()
            S.stt("dve", tmp[:, 0:n], xt[:, kc, 0:n], self.modA[:, j, c, kc:kc + 1], sd[:, 0:n], ALU.mult, ALU.mult,
                  reads=[xtok, sdtok, self.modtok], writes=[ttok])
            S.act(h[:, kc, 0:n], tmp[:, 0:n], AF.Identity, reads=[ttok, self.modtok], writes=[htok],
                  bias=self.modraw[:, 3 * j, kc, c:c + 1], scale=1.0)

    def phase_adaln(self, l):
        S, nc = self.S, self.nc
        with ExitStack() as es:
            self.es = es
            stage = Ring(self, "adst", 2, [128, 8, 1024], F32)
            sc = self.sb("ad_sc", [128, 8, 2])
            sctok = S.tok()
            pst = self.ps("ad_ps", [128, 8, 2])
            pstok = S.tok()
            S.act(sc[:], self.vec_c[:], AF.Silu, writes=[sctok])
            awv = self.ada_w[l].rearrange("(kc p) n -> p kc n", p=128)
            for i in range(9):
                st, stok = stage.next()
                S.dma("sp", st[:, 0:4, :], awv[:, 0:4, i * 1024:(i + 1) * 1024], writes=[stok])
                S.dma("pool", st[:, 4:8, :], awv[:, 4:8, i * 1024:(i + 1) * 1024], writes=[stok])
                for oc in range(8):
                    for kc in range(8):
                        S.mm(pst[:, oc, :], st[:, kc, oc * 128:(oc + 1) * 128], sc[:, kc, :], kc == 0, kc == 7,
                             reads=[stok, sctok], writes=[pstok])
                for c in range(2):
                    S.tt("dve", self.modraw[:, i, :, c], pst[:, :, c], self.vec_adab[:, l, i, :], ALU.add,
                         reads=[pstok], writes=[self.modtok])
                S.maybe_flush()
            for j in range(3):
                for c in range(2):
                    S.stt("dve", self.modA[:, j, c, :], self.modraw[:, 3 * j + 1, :, c], 1.0, self.vec_normw[:, l, j, :],
                          ALU.add, ALU.mult, reads=[self.modtok], writes=[self.modtok])
                    S.ts("dve", self.modG[:, j, c, :], self.modraw[:, 3 * j + 2, :, c], 0.5 if j != 1 else 1.0, None, ALU.mult,
                         reads=[self.modtok], writes=[self.modtok])
            S.flush()

    def phase_ffn(self, l, f, src, dst, last=False):
        S, nc = self.S, self.nc
        j = 0 if f == 0 else 2
        N = 256
        with ExitStack() as es:
            self.es = es
            win = self.sb("f_win", [128, 8, 2 * DFF], BF16)
            wout = self.sb("f_wout", [128, 22, D], BF16)
            self.wtok = S.tok()
            stage = Ring(self, "f_st", 4, [128, 704], F32)
            xr = Ring(self, "f_x", 3, [128, 8, N], F32)
            sqr = Ring(self, "f_sq", 3, [128, N], F32)
            sd = self.sb("f_sd", [128, N])
            sdtok = S.tok()
            hr = Ring(self, "f_h", 2, [128, 8, N], BF16)
            ar = Ring(self, "f_a", 1, [128, 22, N], BF16)
            sgr = Ring(self, "f_sg", 2, [128, N], F32)
            ps_ms = self.ps("f_pms", [128, 512])
            ps_ms_tok = S.tok()
            pg = Ring(self, "f_pg", 2, [128, 512], F32, psum=True)
            pu = Ring(self, "f_pu", 2, [128, 512], F32, psum=True)
            po = Ring(self, "f_po", 2, [128, 512], F32, psum=True)
            self.load_w(win, self.ffn_w_in[l, f], 8, 2 * DFF, stage, 704)
            self.load_w(wout, self.ffn_w_out[l, f], 22, D, stage, 512)
            wtok = self.wtok
            tl = [t for t in self.tiles(N) if not (last and t[2])]

            def prep(ti):
                t0, n, isctx = tl[ti]
                xt, xtok = xr.next()
                S.dma("sp", xt[:, :, 0:n], src[:, t0:t0 + n].rearrange("(kc p) t -> p kc t", p=128), writes=[xtok])
                h, htok = hr.next()
                self.modulate(xt, xtok, n, j, 1 if isctx else 0, h, htok, sqr, ps_ms, ps_ms_tok, sd, sdtok)
                return xt, xtok, h, htok

            nxt = prep(0)
            for ti, (t0, n, isctx) in enumerate(tl):
                c = 1 if isctx else 0
                xt, xtok, h, htok = nxt
                a, atok = ar.next()
                for m in range(22):
                    g, gtok = pg.next()
                    u, utok = pu.next()
                    for kc in range(8):
                        S.mm(g[:, 0:n], win[:, kc, m * 128:(m + 1) * 128], h[:, kc, 0:n], kc == 0, kc == 7,
                             reads=[wtok, htok], writes=[gtok])
                    for kc in range(8):
                        S.mm(u[:, 0:n], win[:, kc, DFF + m * 128:DFF + (m + 1) * 128], h[:, kc, 0:n], kc == 0, kc == 7,
                             reads=[wtok, htok], writes=[utok])
                    sg, sgtok = sgr.next()
                    S.act(sg[:, 0:n], g[:, 0:n], AF.Silu, reads=[gtok], writes=[sgtok])
                    S.tt("dve", a[:, m, 0:n], sg[:, 0:n], u[:, 0:n], ALU.mult, reads=[sgtok, utok], writes=[atok])
                    if m == 12 and ti + 1 < len(tl):
                        nxt = prep(ti + 1)
                for oc in range(8):
                    o, otok = po.next()
                    for m in range(22):
                        S.mm(o[:, 0:n], wout[:, m, oc * 128:(oc + 1) * 128], a[:, m, 0:n], m == 0, m == 21,
                             reads=[wtok, atok], writes=[otok])
                    S.stt("dve", xt[:, oc, 0:n], o[:, 0:n], self.modG[:, j, c, oc:oc + 1], xt[:, oc, 0:n], ALU.mult, ALU.add,
                          reads=[otok, self.modtok, xtok], writes=[xtok])
                if last:
                    S.dma("pool", dst[:, t0 - L_CTX:t0 - L_CTX + n].rearrange("(kc p) t -> p kc t", p=128), xt[:, :, 0:n], reads=[xtok])
                else:
                    S.dma("pool", dst[:, t0:t0 + n].rearrange("(kc p) t -> p kc t", p=128), xt[:, :, 0:n], reads=[xtok])
                S.maybe_flush()
            S.flush()


    def phase_inproj(self, l):
        S = self.S
        with ExitStack() as es:
            self.es = es
            win = self.sb("i_win", [128, 8, NIN], BF16)
            self.wtok = S.tok()
            wtok = self.wtok
            stage = Ring(self, "i_st", 4, [128, 1476], F32)
            xr = Ring(self, "i_x", 2, [128, 8, 512], F32)
            hr = Ring(self, "i_h", 2, [128, 8, 512], BF16)
            sqr = Ring(self, "i_sq", 2, [128, 512], F32)
            sd = self.sb("i_sd", [128, 512])
            sdtok = S.tok()
            o32 = Ring(self, "i_o32", 3, [128, 512], F32)
            o16 = Ring(self, "i_o16", 3, [128, 512], BF16)
            qnr = Ring(self, "i_qn", 3, [128, 512], F32)
            rsr = Ring(self, "i_rs", 3, [128, 512], F32)
            csr = Ring(self, "i_cs", 2, [128, 2, 512], F32)
            ps_ms = self.ps("i_pms", [128, 512])
            ps_ms_tok = S.tok()
            pmain = Ring(self, "i_pm", 4, [128, 512], F32, psum=True)
            pstat = Ring(self, "i_pst", 2, [128, 512], F32, psum=True)
            prot = Ring(self, "i_prot", 1, [128, 512], F32, psum=True)
            self.load_w(win, self.mix_w_in[l], 8, NIN, stage, 1476)
            FM = []
            for i in range(2):
                FM.append((128 * i, "raw", self.MLKT, 128 * i, 0))
            for i in range(3):
                FM.append((528 + 128 * i, "norm", self.NAKT, 128 * i, 1))
            FM.append((1296, "rope", self.GQKT, 0, 3))
            for i in range(2):
                FM.append((1552 + 128 * i, "raw", self.MLQT, 128 * i, 0))
            for i in range(2):
                FM.append((1808 + 128 * i, "sig", self.SIGOT, 128 * i, 0))
            for i in range(3):
                FM.append((2064 + 128 * i, "norm", self.NAQT, 128 * i, 0))
            for i in range(3):
                FM.append((2448 + 128 * i, "rope", self.GQQT, 128 * i, 2))
            for i in range(24):
                FM.append((2832 + 128 * i, "sig", self.BRGT, 128 * i, 0))
            tl = self.tiles(512)
            self._dq = 0

            def nextq():
                self._dq += 1
                return "sp" if self._dq % 2 == 0 else "pool"

            def prep(ti):
                t0, n, isctx = tl[ti]
                xt, xtok = xr.next()
                S.dma("sp", xt[:, :, 0:n], self.XT[:, t0:t0 + n].rearrange("(kc p) t -> p kc t", p=128), writes=[xtok])
                h, htok = hr.next()
                self.modulate(xt, xtok, n, 1, 1 if isctx else 0, h, htok, sqr, ps_ms, ps_ms_tok, sd, sdtok)
                cs, cstok = (None, None)
                if not isctx:
                    cs, cstok = csr.next()
                    S.dma("pool", cs[:, :, 0:n], self.rope_d[:, :, t0 - L_CTX:t0 - L_CTX + n].rearrange("c p t -> p c t"), writes=[cstok])
                return h, htok, cs, cstok

            def epilogue(kind, dsl, widx, p, ptok, n, isctx, cs, cstok):
                q = nextq()
                if kind == "raw" or kind == "sig":
                    o, otok = o32.next()
                    S.act(o[:, 0:n], p[:, 0:n], AF.Copy if kind == "raw" else AF.Sigmoid, reads=[ptok], writes=[otok])
                    S.dma(q, dsl, o[:, 0:n], reads=[otok])
                    return
                sq, sqtok = rsr.next()
                S.act(sq[:, 0:n], p[:, 0:n], AF.Square, reads=[ptok], writes=[sqtok])
                qn, qntok = qnr.next()
                S.act(qn[:, 0:n], p[:, 0:n], AF.Copy, reads=[ptok], writes=[qntok])
                pst, psttok = pstat.next()
                S.mm(pst[:, 0:n], self.c_ones64, sq[:, 0:n], True, True, reads=[sqtok], writes=[psttok])
                rs, rstok = rsr.next()
                S.act(rs[:, 0:n], pst[:, 0:n], AF.Sqrt, reads=[psttok], writes=[rstok], bias=self.c_eps, scale=1.0)
                S.recip(rs[:, 0:n], rs[:, 0:n], reads=[rstok], writes=[rstok])
                wq = self.vec_qkw[:, l, widx:widx + 1]
                if kind == "norm" or isctx:
                    o, otok = o16.next()
                    S.stt("dve", o[:, 0:n], qn[:, 0:n], wq, rs[:, 0:n], ALU.mult, ALU.mult, reads=[qntok, rstok], writes=[otok])
                    S.dma(q, dsl, o[:, 0:n], reads=[otok])
                    return
                S.stt("dve", qn[:, 0:n], qn[:, 0:n], wq, rs[:, 0:n], ALU.mult, ALU.mult, reads=[qntok, rstok], writes=[qntok])
                pr, prtok = prot.next()
                S.mm(pr[:, 0:n], self.c_rot, qn[:, 0:n], True, True, reads=[qntok], writes=[prtok])
                t1, t1tok = o32.next()
                S.tt("dve", t1[:, 0:n], qn[:, 0:n], cs[:, 0, 0:n], ALU.mult, reads=[qntok, cstok], writes=[t1tok])
                S.tt("dve", qn[:, 0:n], pr[:, 0:n], cs[:, 1, 0:n], ALU.mult, reads=[prtok, cstok], writes=[qntok])
                o, otok = o16.next()
                S.tt("dve", o[:, 0:n], t1[:, 0:n], qn[:, 0:n], ALU.add, reads=[t1tok, qntok], writes=[otok])
                S.dma(q, dsl, o[:, 0:n], reads=[otok])

            def tm_epilogue(p, ptok, cw, dst, is16, tt0):
                o, otok = (o16 if is16 else o32).next()
                S.copy("dve", o[:, 0:cw], p[:, 0:cw], reads=[ptok], writes=[otok])
                S.dma(nextq(), dst[tt0:tt0 + 128, 0:cw], o[:, 0:cw], reads=[otok])

            nxt = prep(0)
            for ti, (t0, n, isctx) in enumerate(tl):
                h, htok, cs, cstok = nxt
                pend = None
                for idx, (col0, kind, dst, row0, widx) in enumerate(FM):
                    p, ptok = pmain.next()
                    for kc in range(8):
                        S.mm(p[:, 0:n], win[:, kc, col0:col0 + 128], h[:, kc, 0:n], kc == 0, kc == 7, reads=[wtok, htok], writes=[ptok])
                    if pend is not None:
                        epilogue(*pend)
                    pend = (kind, dst[row0:row0 + 128, t0:t0 + n], widx, p, ptok, n, isctx, cs, cstok)
                    if idx == 24 and ti + 1 < len(tl):
                        nxt = prep(ti + 1)
                tpend = None
                for s_ in range(n // 128):
                    tt0 = t0 + s_ * 128
                    for (c0, cw, dst, is16) in ((0, 512, self.MLKV, False), (512, 16, self.MLG, False),
                                                (912, 384, self.NAV, True), (1424, 128, self.GQV, True)):
                        p, ptok = pmain.next()
                        for kc in range(8):
                            S.mm(p[:, 0:cw], h[:, kc, s_ * 128:(s_ + 1) * 128], win[:, kc, c0:c0 + cw], kc == 0, kc == 7,
                                 reads=[wtok, htok], writes=[ptok])
                        if pend is not None:
                            epilogue(*pend)
                            pend = None
                        if tpend is not None:
                            tm_epilogue(*tpend)
                        tpend = (p, ptok, cw, dst, is16, tt0)
                tm_epilogue(*tpend)
                S.maybe_flush()
            S.flush()

    def phase_biasprep(self):
        S = self.S
        with ExitStack() as es:
            self.es = es
            rm = self.sb("b_rm", [128, 3, 8, 512], F32)
            rmtok = S.tok()
            S.dma("sp", rm[:], self.rowmask_d.rearrange("c k p q -> p c k q"), writes=[rmtok])
            tr = Ring(self, "b_t", 4, [128, 8, 64], F32)
            i = 0
            for l in range(self.layers):
                for h in range(6):
                    for case, delta in ((0, 0), (1, -4), (2, -8)):
                        for c in range(8):
                            t, ttok = tr.next()
                            for a in range(2):
                                r0 = 15 - 2 * c - a - delta
                                S.dma("sp" if i % 2 == 0 else "pool", t[a * 64:(a + 1) * 64, :, :],
                                      self.rpbT_d[l, h, r0:r0 + 8].rearrange("j k q -> k j q"), writes=[ttok])
                                i += 1
                            tv = t[:].rearrange("p j q -> p (j q)")
                            S.tt("pool", tv, tv, rm[:, case, c, :], ALU.add, reads=[ttok, rmtok], writes=[ttok])
                            S.act(tv, tv, AF.Identity, reads=[ttok], writes=[ttok], scale=8.0)
                            S.dma("act", self.BT[l, h, case, c], tv, reads=[ttok])
                        S.maybe_flush()
            S.flush()

    def phase_mlstm(self, l):
        S = self.S
        NCH = self.NCH
        with ExitStack() as es:
            self.es = es
            TRI = [self.cst[:, 512:640], self.cst[:, 640:768]]
            NTRI = [self.cst[:, 768:896], self.cst[:, 896:1024]]
            IDN = self.cst[:, 384:512]
            ones64f = self.cst[:, 1088:1152]
            gr = Ring(self, "m_g", 3, [128, 16], F32)
            kvr = Ring(self, "m_kv", 3, [128, 512], F32)
            qkr = Ring(self, "m_qk", 3, [128, 4, 128], F32)
            g2r = Ring(self, "m_g2", 3, [128, 8], F32)
            e1r = Ring(self, "m_e1", 3, [128, 8], F32)
            repr_ = Ring(self, "m_rep", 3, [128, 8, 64], F32)
            kpr = Ring(self, "m_kp", 3, [128, 256], F32)
            vbr = Ring(self, "m_vb", 3, [128, 2, 256], F32)
            ebr = Ring(self, "m_eb", 8, [128, 256], F32)
            qpr = Ring(self, "m_qp", 4, [128, 2, 128], F32)
            kptr = Ring(self, "m_kpt", 4, [128, 2, 128], F32)
            atr = Ring(self, "m_at", 4, [128, 128], F32)
            adr = Ring(self, "m_ad", 4, [64, 128], F32)
            hor = Ring(self, "m_ho", 4, [64, 128], F32)
            tmpr = Ring(self, "m_tmp", 3, [128, 256], F32)
            St = [[(self.sb("m_st", [128, 256]), S.tok()) for hp in range(2)] for d in range(2)]
            Stb = [[(self.sb("m_stb", [128, 256], F32), S.tok()) for hp in range(2)] for d in range(2)]
            pab = Ring(self, "m_pab", 2, [128, 256], F32, psum=True)
            psm = self.ps("m_psm", [128, 8])
            psmtok = S.tok()
            pSr = Ring(self, "m_pS", 2, [128, 128], F32, psum=True)
            pNr = Ring(self, "m_pN", 2, [128, 128], F32, psum=True)
            pUr = Ring(self, "m_pU", 1, [128, 256], F32, psum=True)
            for d in range(2):
                for hp in range(2):
                    S.memset("dve", St[d][hp][0][:], 0.0, writes=[St[d][hp][1]])
                    S.memset("dve", Stb[d][hp][0][:], 0.0, writes=[Stb[d][hp][1]])
            for (vb, vbtok) in vbr.bufs:
                vb4 = vb[:].rearrange("p a (h c) -> p a h c", h=2)
                S.memset("dve", vb4[:, :, :, 64:128], 1.0, writes=[vbtok])
            self._dq = 0
            order_f = list(range(NCH))
            order_b = [1, 0] + list(range(NCH - 1, 1, -1))
            dq = 0
            def unit(d, c):
                if True:
                    t0 = 128 * c
                    last = 127 if d == 0 else 0
                    HT = self.HFT if d == 0 else self.HBT
                    gt, gtok = gr.next()
                    S.dma("sp", gt[:], self.MLG[t0:t0 + 128, :], writes=[gtok])
                    kv, kvtok = kvr.next()
                    S.dma("pool", kv[:], self.MLKV[t0:t0 + 128, :], writes=[kvtok])
                    qk, qktok = qkr.next()
                    S.dma("sp", qk[:, 0:2, :], self.MLQT[:, t0:t0 + 128].rearrange("(c p) t -> p c t", p=128), writes=[qktok])
                    S.dma("pool", qk[:, 2:4, :], self.MLKT[:, t0:t0 + 128].rearrange("(c p) t -> p c t", p=128), writes=[qktok])
                    g2, g2tok = g2r.next()
                    S.tt("dve", g2[:], gt[:, 8 * d:8 * d + 8], self.vec_gateb[:, l, 8 * d:8 * d + 8], ALU.add, reads=[gtok], writes=[g2tok])
                    e1, e1tok = e1r.next()
                    S.act(e1[:, 0:4], g2[:, 4:8], AF.Exp, reads=[g2tok], writes=[e1tok], scale=-1.0)
                    S.act(e1[:, 4:8], e1[:, 0:4], AF.Ln, reads=[e1tok], writes=[e1tok], bias=self.c_one, scale=1.0)
                    S.ts("dve", g2[:, 4:8], e1[:, 4:8], -1.0, None, ALU.mult, reads=[e1tok], writes=[g2tok])
                    yield
                    rep, reptok = repr_.next()
                    for i in range(8):
                        if i % 2 == 0:
                            S.ts("dve", rep[:, i, :], ones64f, g2[:, i:i + 1], None, ALU.mult, reads=[g2tok], writes=[reptok])
                        else:
                            S.act(rep[:, i, :], ones64f, AF.Identity, reads=[g2tok], writes=[reptok], scale=g2[:, i:i + 1])
                    yield
                    S.mm(psm[:, 0:4], TRI[d], g2[:, 4:8], True, True, reads=[g2tok], writes=[psmtok])
                    S.tt("dve", e1[:, 0:4], g2[:, 0:4], psm[:, 0:4], ALU.subtract, reads=[g2tok, psmtok], writes=[e1tok])
                    S.act(e1[:, 0:4], e1[:, 0:4], AF.Exp, reads=[e1tok], writes=[e1tok])
                    kp, kptok = kpr.next()
                    for h in range(4):
                        S.ts("dve", kp[:, h * 64:(h + 1) * 64], kv[:, h * 64:(h + 1) * 64], e1[:, h:h + 1], 0.125, ALU.mult, ALU.mult,
                             reads=[kvtok, e1tok], writes=[kptok])
                    yield
                    vb, vbtok = vbr.next()
                    S.copy("act", vb[:].rearrange("p a (h c) -> p a h c", h=2)[:, :, :, 0:64],
                           kv[:, 256:512].rearrange("p (a h c) -> p a h c", a=2, h=2), reads=[kvtok], writes=[vbtok])
                    qp, qptok = qpr.next()
                    kpt, kpttok = kptr.next()
                    for hp in range(2):
                        lhs_lf = rep[:, 4 + 2 * hp:6 + 2 * hp, :].rearrange("p a b -> p (a b)")
                        lhs_ig = rep[:, 2 * hp:2 * hp + 2, :].rearrange("p a b -> p (a b)")
                        pa, patok = pab.next()
                        S.mm(pa[:, 0:128], lhs_lf, TRI[d], True, True, reads=[reptok], writes=[patok])
                        S.mm(pa[:, 128:256], lhs_ig, IDN, True, False, reads=[reptok], writes=[patok])
                        S.mm(pa[:, 128:256], lhs_lf, NTRI[d], False, True, reads=[reptok], writes=[patok])
                        yield
                        eb, ebtok = ebr.next()
                        S.act(eb[:], pa[:], AF.Exp, reads=[patok], writes=[ebtok])
                        S.tt("dve", qp[:, hp, :], qk[:, hp, :], eb[:, 0:128], ALU.mult, reads=[qktok, ebtok], writes=[qptok])
                        S.stt("dve", kpt[:, hp, :], qk[:, 2 + hp, :], 0.125, eb[:, 128:256], ALU.mult, ALU.mult, reads=[qktok, ebtok], writes=[kpttok])
                        ebt = eb[:, last:last + 1]
                        yield
                        st, sttok = St[d][hp]
                        stb, stbtok = Stb[d][hp]
                        for hh in range(2):
                            h = 2 * hp + hh
                            hb = 64 * hh
                            pS, pStok = pSr.next()
                            S.mm(pS[:], kpt[hb:hb + 64, hp, :], qp[hb:hb + 64, hp, :], True, True, reads=[kpttok, qptok], writes=[pStok])
                            yield
                            at, attok = atr.next()
                            S.tt("dve", at[:], pS[:], TRI[d], ALU.mult, reads=[pStok], writes=[attok])
                            pN, pNtok = pNr.next()
                            S.mm(pN[:, :], vb[:, hp, hh * 128:(hh + 1) * 128], at[:], True, False, reads=[vbtok, attok], writes=[pNtok])
                            S.mm(pN[:, :], stb[hb:hb + 64, hh * 128:(hh + 1) * 128], qp[hb:hb + 64, hp, :], False, True, reads=[stbtok, qptok], writes=[pNtok])
                            yield
                            ad, adtok = adr.next()
                            S.act(ad[:], pN[64:128, :], AF.Abs, reads=[pNtok], writes=[adtok])
                            S.ts("dve", ad[:], ad[:], 1.0, None, ALU.max, reads=[adtok], writes=[adtok])
                            S.recip(ad[:], ad[:], reads=[adtok], writes=[adtok])
                            ho, hotok = hor.next()
                            S.tt("dve", ho[:], pN[0:64, :], ad[:], ALU.mult, reads=[pNtok, adtok], writes=[hotok])
                            S.dma("sp" if self._dq % 2 == 0 else "pool", HT[h * 64:(h + 1) * 64, t0:t0 + 128], ho[:], reads=[hotok])
                            self._dq += 1
                        pU, pUtok = pUr.next()
                        S.mm(pU[:], kp[:, hp * 128:(hp + 1) * 128], vb[:, hp, :], True, True, reads=[kptok, vbtok], writes=[pUtok])
                        tmp, tmptok = tmpr.next()
                        S.tt("dve", tmp[:], st[:], pU[:], ALU.add, reads=[sttok, pUtok], writes=[tmptok])
                        S.ts("dve", st[:], tmp[:], ebt, None, ALU.mult, reads=[tmptok, ebtok], writes=[sttok])
                        S.act(stb[:], tmp[:], AF.Identity, reads=[tmptok, ebtok], writes=[stbtok], scale=ebt)
                        yield

            for step in range(NCH):
                gens = [unit(0, order_f[step]), unit(1, order_b[step])]
                while gens:
                    for g_ in list(gens):
                        try:
                            next(g_)
                        except StopIteration:
                            gens.remove(g_)
                S.maybe_flush()
            S.flush()

    def attn_res(self):
        S = self.S
        NCH, T = self.NCH, self.T
        kt_g = self.sb("a_ktg", [128, 2, T], BF16)
        kt_n = self.sb("a_ktn", [128, 3, T], BF16)
        vg = self.sb("a_vg", [128, NCH, 2, 128], BF16)
        vn = self.sb("a_vn", [128, NCH, 6, 128], BF16)
        rtok = S.tok()
        S.memset("dve", vg[:, :, :, 64:128], 1.0, writes=[rtok])
        S.memset("pool", vn[:, :, :, 64:128], 1.0, writes=[rtok])
        for kv in range(2):
            for half in range(2):
                S.dma("sp", kt_g[half * 64:(half + 1) * 64, kv, :], self.GQKT[kv * 64:(kv + 1) * 64, :], writes=[rtok])
        for i in range(3):
            S.dma("sp", kt_n[:, i, :], self.NAKT[i * 128:(i + 1) * 128, :], writes=[rtok])
        gqv = self.GQV.rearrange("(c p) (h d) -> p c h d", p=128, d=64)
        nav = self.NAV.rearrange("(c p) (h d) -> p c h d", p=128, d=64)
        for h in range(2):
            S.dma("pool", vg[:, :, h, 0:64], gqv[:, :, h, :], writes=[rtok])
        for h in range(6):
            S.dma("pool" if h % 2 == 0 else "act", vn[:, :, h, 0:64], nav[:, :, h, :], writes=[rtok])
        return kt_g, kt_n, vg, vn, rtok

    def phase_mixers(self, l, ctx_out):
        with ExitStack() as outer:
            self.es = outer
            res = self.attn_res()
            self.phase_mlstm(l)
            self.phase_attn(l, ctx_out, res)

    def phase_attn(self, l, ctx_out, res):
        S = self.S
        NCH, T, NB, NR = self.NCH, self.T, self.NB, self.NR
        with ExitStack() as es:
            self.es = es
            kt_g, kt_n, vg, vn, rtok = res
            qrs = [Ring(self, "a_q", 2, [128, 512], BF16) for _ in range(2)]
            for par in range(2):
                for (qb, qbtok) in qrs[par].bufs:
                    S.memset("dve", qb[:], 0.0, writes=[qbtok])
            btr = Ring(self, "a_bt", 2, [128, 8, 512], F32)
            G = 3
            ptr = Ring(self, "a_pt", 3, [128, G, 512], BF16)
            rdr = Ring(self, "a_rd", 2, [64, 512], F32)
            obr = Ring(self, "a_ob", 2, [64, 512], BF16)
            pSr = Ring(self, "a_pS", 2, [128, G, 512], F32, psum=True)
            pNr = Ring(self, "a_pN", 2, [128, 512], F32, psum=True)
            self._dq = 0

            def attend(qsrc, h, hb, tq0, n, kt, ktc, keychunks, vfn, dst):
                q, qtok = qrs[hb // 64].next()
                S.dma("sp", q[hb:hb + 64, 0:n], qsrc[h * 64:(h + 1) * 64, tq0:tq0 + n], writes=[qtok])
                pn, pntok = pNr.next()
                nk = len(keychunks)
                groups = [list(range(i, min(i + G, nk))) for i in range(0, nk, G)]
                pst = [None] * len(groups)

                def qk(gi):
                    ps_, pstok = pSr.next()
                    for jj, i in enumerate(groups[gi]):
                        kc, bias, btok = keychunks[i]
                        S.mm(ps_[:, jj, 0:n], kt[:, ktc, kc * 128:(kc + 1) * 128], q[:, 0:n], True, bias is None,
                             reads=[rtok, qtok], writes=[pstok])
                        if bias is not None:
                            S.mm(ps_[:, jj, 0:n], self.cst[:, 384:512], bias[:, 0:n], False, True, reads=[btok], writes=[pstok])
                    pst[gi] = (ps_, pstok)

                qk(0)
                for gi, grp in enumerate(groups):
                    if gi + 1 < len(groups):
                        qk(gi + 1)
                    ps_, pstok = pst[gi]
                    pt, pttok = ptr.next()
                    S.act(pt[:, 0:len(grp), 0:n], ps_[:, 0:len(grp), 0:n], AF.Exp, reads=[pstok], writes=[pttok], scale=0.125)
                    for jj, i in enumerate(grp):
                        kc = keychunks[i][0]
                        S.mm(pn[:, 0:n], vfn(kc), pt[:, jj, 0:n], i == 0, i == nk - 1, reads=[rtok, pttok], writes=[pntok])
                rd, rdtok = rdr.next()
                S.act(rd[:, 0:n], pn[64:128, 0:n], AF.Copy, reads=[pntok], writes=[rdtok])
                S.recip(rd[:, 0:n], rd[:, 0:n], reads=[rdtok], writes=[rdtok])
                ob, obtok = obr.next()
                S.tt("dve", ob[:, 0:n], pn[0:64, 0:n], rd[:, 0:n], ALU.mult, reads=[pntok, rdtok], writes=[obtok])
                S.dma("pool" if self._dq % 2 == 0 else "sp", dst[h * 64:(h + 1) * 64, tq0:tq0 + n], ob[:, 0:n], reads=[obtok])
                self._dq += 1
                S.maybe_flush()

            ctxk = [(0, None, None), (1, None, None)]
            for h in range(6):
                hb = (h % 2) * 64
                vn_fn = lambda kc, h=h: vn[:, kc, h, :]
                cur = None
                for B in range(NB):
                    kw = min(max(8 * B - 4, 0), NR - 16)
                    case = {0: 0, -4: 1, -8: 2}[kw - 8 * B]
                    if case != cur:
                        bt, bttok = btr.next()
                        S.dma("sp", bt[:], self.BT[l, h, case].rearrange("c p q -> p c q"), writes=[bttok])
                        cur = case
                    keys = [(2 + (kw + 2 * c) // 2, bt[:, c, :], bttok) for c in range(8)] + ctxk
                    attend(self.NAQT, h, hb, L_CTX + 512 * B, 512, kt_n, h // 2, keys, vn_fn, self.ONAT)
                if ctx_out:
                    attend(self.NAQT, h, hb, 0, L_CTX, kt_n, h // 2, ctxk, vn_fn, self.ONAT)
            allk = [(kc, None, None) for kc in range(NCH)]
            for h in range(6):
                hb = (h % 2) * 64
                kvh = h // 3
                vg_fn = lambda kc, kvh=kvh: vg[:, kc, kvh, :]
                for B in range(NB):
                    attend(self.GQQT, h, hb, L_CTX + 512 * B, 512, kt_g, kvh, allk, vg_fn, self.OGQT)
                if ctx_out:
                    attend(self.GQQT, h, hb, 0, L_CTX, kt_g, kvh, ctxk, vg_fn, self.OGQT)
            S.flush()

    def phase_merge(self, l, ctx_out):
        S = self.S
        with ExitStack() as es:
            self.es = es
            wml = self.sb("g_wml", [128, 2, D], BF16)
            wna = self.sb("g_wna", [128, 3, D], BF16)
            wgq = self.sb("g_wgq", [128, 3, D], BF16)
            wo = self.sb("g_wo", [128, 8, D], BF16)
            self.wtok = S.tok()
            wtok = self.wtok
            stage = Ring(self, "g_st", 4, [128, 1024], F32)
            self.load_w(wml, self.w_br_ml[l], 2, D, stage, 1024)
            self.load_w(wna, self.w_br_na[l], 3, D, stage, 1024)
            self.load_w(wgq, self.w_br_gq[l], 3, D, stage, 1024)
            self.load_w(wo, self.w_out[l], 8, D, stage, 1024)
            hfr = Ring(self, "g_hf", 2, [128, 3, 2, 512], F32)
            onr = Ring(self, "g_on", 2, [128, 2, 3, 512], BF16)
            xr = Ring(self, "g_x", 2, [128, 8, 512], F32)
            hsr = Ring(self, "g_hs", 2, [128, 512], F32)
            sqr = Ring(self, "g_sq", 2, [128, 512], F32)
            omr = Ring(self, "g_om", 1, [128, 2, 512], BF16)
            ggr = Ring(self, "g_gg", 8, [128, 3, 512], F32)
            t1r = Ring(self, "g_t1", 2, [128, 512], F32)
            t2r = Ring(self, "g_t2", 2, [128, 512], F32)
            yr = Ring(self, "g_y", 1, [128, 8, 512], BF16)
            pst = Ring(self, "g_pst", 1, [128, 512], F32, psum=True)
            p1r = Ring(self, "g_p1", 2, [128, 512], F32, psum=True)
            p2r = Ring(self, "g_p2", 2, [128, 512], F32, psum=True)
            p3r = Ring(self, "g_p3", 2, [128, 512], F32, psum=True)
            por = Ring(self, "g_po", 1, [128, 512], F32, psum=True)
            tl = [t for t in self.tiles(512) if not (t[2] and not ctx_out)]
            brg4 = lambda t0, n: self.BRGT[:, t0:t0 + n].rearrange("(b o p) t -> p b o t", b=3, p=128)

            def mload(ti):
                t0, n, isctx = tl[ti]
                hf, hftok = hfr.next()
                for i, src in enumerate((self.HFT, self.HBT, self.SIGOT)):
                    S.dma("sp", hf[:, i, :, 0:n], src[:, t0:t0 + n].rearrange("(c p) t -> p c t", p=128), writes=[hftok])
                on, ontok = onr.next()
                for i, src in enumerate((self.ONAT, self.OGQT)):
                    S.dma("pool", on[:, i, :, 0:n], src[:, t0:t0 + n].rearrange("(c p) t -> p c t", p=128), writes=[ontok])
                xt, xtok = xr.next()
                S.dma("sp", xt[:, :, 0:n], self.XT[:, t0:t0 + n].rearrange("(kc p) t -> p kc t", p=128), writes=[xtok])
                return hf, hftok, on, ontok, xt, xtok

            def gload(ti, oc):
                t0, n, isctx = tl[ti]
                gg, ggtok = ggr.next()
                S.dma(("pool", "act", "sp")[oc % 3], gg[:, :, 0:n], brg4(t0, n)[:, :, oc, :], writes=[ggtok])
                return gg, ggtok

            nxt = mload(0)
            gq = [gload(0, oc) for oc in range(6)]
            for ti, (t0, n, isctx) in enumerate(tl):
                c = 1 if isctx else 0
                hf, hftok, on, ontok, xt, xtok = nxt
                om, omtok = omr.next()
                for cc in range(2):
                    hs, hstok = hsr.next()
                    S.tt("dve", hs[:, 0:n], hf[:, 0, cc, 0:n], hf[:, 1, cc, 0:n], ALU.add, reads=[hftok], writes=[hstok])
                    sq, sqtok = sqr.next()
                    S.act(sq[:, 0:n], hs[:, 0:n], AF.Square, reads=[hstok], writes=[sqtok])
                    ps_, pstok = pst.next()
                    S.mm(ps_[:, 0:n], self.c_ones64, sq[:, 0:n], True, True, reads=[sqtok], writes=[pstok])
                    S.act(sq[:, 0:n], ps_[:, 0:n], AF.Sqrt, reads=[pstok], writes=[sqtok], bias=self.c_eps, scale=1.0)
                    S.recip(sq[:, 0:n], sq[:, 0:n], reads=[sqtok], writes=[sqtok])
                    S.stt("dve", hs[:, 0:n], hs[:, 0:n], self.vec_mlnw[:, l, cc:cc + 1], sq[:, 0:n], ALU.mult, ALU.mult, reads=[hstok, sqtok], writes=[hstok])
                    S.tt("dve", om[:, cc, 0:n], hs[:, 0:n], hf[:, 2, cc, 0:n], ALU.mult, reads=[hstok, hftok], writes=[omtok])
                y, ytok = yr.next()
                for oc in range(8):
                    gg, ggtok = gq.pop(0)
                    nti, noc = (ti, oc + 6) if oc + 6 < 8 else (ti + 1, oc + 6 - 8)
                    if nti < len(tl):
                        gq.append(gload(nti, noc))
                    if oc == 2 and ti + 1 < len(tl):
                        nxt = mload(ti + 1)
                    p1, p1tok = p1r.next()
                    p2, p2tok = p2r.next()
                    p3, p3tok = p3r.next()
                    osl = slice(oc * 128, (oc + 1) * 128)
                    for cc in range(2):
                        S.mm(p1[:, 0:n], wml[:, cc, osl], om[:, cc, 0:n], cc == 0, cc == 1, reads=[wtok, omtok], writes=[p1tok])
                    for cc in range(3):
                        S.mm(p2[:, 0:n], wna[:, cc, osl], on[:, 0, cc, 0:n], cc == 0, cc == 2, reads=[wtok, ontok], writes=[p2tok])
                    for cc in range(3):
                        S.mm(p3[:, 0:n], wgq[:, cc, osl], on[:, 1, cc, 0:n], cc == 0, cc == 2, reads=[wtok, ontok], writes=[p3tok])
                    t1, t1tok = t1r.next()
                    t2, t2tok = t2r.next()
                    S.tt("dve", t1[:, 0:n], p1[:, 0:n], gg[:, 0, 0:n], ALU.mult, reads=[p1tok, ggtok], writes=[t1tok])
                    S.tt("dve", t2[:, 0:n], p2[:, 0:n], gg[:, 1, 0:n], ALU.mult, reads=[p2tok, ggtok], writes=[t2tok])
                    S.tt("pool", t1[:, 0:n], t1[:, 0:n], t2[:, 0:n], ALU.add, reads=[t1tok, t2tok], writes=[t1tok])
                    S.tt("dve", t2[:, 0:n], p3[:, 0:n], gg[:, 2, 0:n], ALU.mult, reads=[p3tok, ggtok, t1tok], writes=[t2tok])
                    S.tt("pool", y[:, oc, 0:n], t1[:, 0:n], t2[:, 0:n], ALU.add, reads=[t1tok, t2tok], writes=[ytok])
                for oc in range(8):
                    po, potok = por.next()
                    for kc in range(8):
                        S.mm(po[:, 0:n], wo[:, kc, oc * 128:(oc + 1) * 128], y[:, kc, 0:n], kc == 0, kc == 7, reads=[wtok, ytok], writes=[potok])
                    S.stt("dve", xt[:, oc, 0:n], po[:, 0:n], self.modG[:, 1, c, oc:oc + 1], xt[:, oc, 0:n], ALU.mult, ALU.add,
                          reads=[potok, self.modtok, xtok], writes=[xtok])
                S.dma("pool", self.XT[:, t0:t0 + n].rearrange("(kc p) t -> p kc t", p=128), xt[:, :, 0:n], reads=[xtok])
                S.maybe_flush()
            S.flush()

    def phase_setup(self):
        S = self.S
        with ExitStack() as es:
            self.es = es
            rm = self.sb("b_rm", [128, 3, 8, 512], F32)
            rmtok = S.tok()
            S.dma("sp", rm[:], self.rowmask_d.rearrange("c k p q -> p c k q"), writes=[rmtok])
            tr = Ring(self, "b_t", 4, [128, 8, 64], F32)
            stage = Ring(self, "adst", 2, [128, 8, 1024], F32)
            sc = self.sb("ad_sc", [128, 8, 2])
            sctok = S.tok()
            pst = self.ps("ad_ps", [128, 8, 2])
            pstok = S.tok()
            S.act(sc[:], self.vec_c[:], AF.Silu, writes=[sctok])

            def bias_gen():
                i = 0
                for l in range(self.layers):
                    for h in range(6):
                        for case, delta in ((0, 0), (1, -4), (2, -8)):
                            for c in range(8):
                                t, ttok = tr.next()
                                for a in range(2):
                                    r0 = 15 - 2 * c - a - delta
                                    S.dma("sp" if i % 2 == 0 else "pool", t[a * 64:(a + 1) * 64, :, :],
                                          self.rpbT_d[l, h, r0:r0 + 8].rearrange("j k q -> k j q"), writes=[ttok])
                                    i += 1
                                tv = t[:].rearrange("p j q -> p (j q)")
                                S.tt("pool", tv, tv, rm[:, case, c, :], ALU.add, reads=[ttok, rmtok], writes=[ttok])
                                S.act(tv, tv, AF.Identity, reads=[ttok], writes=[ttok], scale=8.0)
                                S.dma("act", self.BT[l, h, case, c], tv, reads=[ttok])
                                yield

            bg = bias_gen()

            def bias_steps(k):
                for _ in range(k):
                    try:
                        next(bg)
                    except StopIteration:
                        return

            for l in range(self.layers):
                modraw, modA, modG = self.modraw_all[:, l], self.modA_all[:, l], self.modG_all[:, l]
                awv = self.ada_w[l].rearrange("(kc p) n -> p kc n", p=128)
                for i in range(9):
                    st, stok = stage.next()
                    S.dma("sp", st[:, 0:4, :], awv[:, 0:4, i * 1024:(i + 1) * 1024], writes=[stok])
                    S.dma("pool", st[:, 4:8, :], awv[:, 4:8, i * 1024:(i + 1) * 1024], writes=[stok])
                    for oc in range(8):
                        for kc in range(8):
                            S.mm(pst[:, oc, :], st[:, kc, oc * 128:(oc + 1) * 128], sc[:, kc, :], kc == 0, kc == 7,
                                 reads=[stok, sctok], writes=[pstok])
                    for c in range(2):
                        S.tt("dve", modraw[:, i, :, c], pst[:, :, c], self.vec_adab[:, l, i, :], ALU.add,
                             reads=[pstok], writes=[self.modtok])
                    bias_steps(16)
                for j in range(3):
                    for c in range(2):
                        S.stt("dve", modA[:, j, c, :], modraw[:, 3 * j + 1, :, c], 1.0, self.vec_normw[:, l, j, :],
                              ALU.add, ALU.mult, reads=[self.modtok], writes=[self.modtok])
                        S.ts("dve", modG[:, j, c, :], modraw[:, 3 * j + 2, :, c], 0.5 if j != 1 else 1.0, None, ALU.mult,
                             reads=[self.modtok], writes=[self.modtok])
            bias_steps(10 ** 6)
            S.flush()

    def build(self):
        nc = self.nc
        T, S_lat, LY = self.T, self.S_lat, self.layers
        NV = 256
        self.xin = self.din("xin", [D, T])
        self.ada_w = self.din("ada_w", [2, D, 9 * D])
        self.ffn_w_in = self.din("ffn_w_in", [2, 2, D, 2 * DFF])
        self.ffn_w_out = self.din("ffn_w_out", [2, 2, DFF, D])
        self.mix_w_in = self.din("mix_w_in", [2, D, NIN])
        self.w_br_ml = self.din("w_br_ml", [2, 256, D])
        self.w_br_na = self.din("w_br_na", [2, 384, D])
        self.w_br_gq = self.din("w_br_gq", [2, 384, D])
        self.w_out = self.din("w_out", [2, D, D])
        self.consts_d = self.din("consts", [128, 1280])
        self.vecs_d = self.din("vecs", [128, NV])
        self.rope_d = self.din("rope", [2, 128, S_lat])
        self.rowmask_d = self.din("rowmask", [3, 8, 128, 512])
        self.rpbT_d = self.din("rpbT", [2, 6, 31, 64, 64])
        self.out = self.nc.dram_tensor("out", [D, S_lat], F32, kind="ExternalOutput").ap()
        self.XT = self.dscr("XT", [D, T])
        self.MLQT = self.dscr("MLQT", [256, T])
        self.MLKT = self.dscr("MLKT", [256, T])
        self.MLKV = self.dscr("MLKV", [T, 512])
        self.MLG = self.dscr("MLG", [T, 16])
        self.SIGOT = self.dscr("SIGOT", [256, T])
        self.NAQT = self.dscr("NAQT", [384, T], BF16)
        self.NAKT = self.dscr("NAKT", [384, T], BF16)
        self.NAV = self.dscr("NAV", [T, 384], BF16)
        self.GQQT = self.dscr("GQQT", [384, T], BF16)
        self.GQKT = self.dscr("GQKT", [128, T], BF16)
        self.GQV = self.dscr("GQV", [T, 128], BF16)
        self.BRGT = self.dscr("BRGT", [3 * D, T])
        self.HFT = self.dscr("HFT", [256, T])
        self.HBT = self.dscr("HBT", [256, T])
        self.ONAT = self.dscr("ONAT", [384, T], BF16)
        self.OGQT = self.dscr("OGQT", [384, T], BF16)
        self.BT = self.nc.dram_tensor("BT", [2, 6, 3, 8, 128, 512], F32).ap()
        with ExitStack() as pes:
            self.S = Sched(nc, pes)
            S = self.S
            self.cst = pes.enter_context(nc.sbuf_tensor("cst_sb", [128, 1280], F32))
            self.vecs = pes.enter_context(nc.sbuf_tensor("vecs_sb", [128, NV], F32))
            self.modraw_all = pes.enter_context(nc.sbuf_tensor("modraw", [128, 2, 9, 8, 2], F32))
            self.modA_all = pes.enter_context(nc.sbuf_tensor("modA", [128, 2, 3, 2, 8], F32))
            self.modG_all = pes.enter_context(nc.sbuf_tensor("modG", [128, 2, 3, 2, 8], F32))
            self.c_onesD = self.cst[:, 0:128]
            self.c_ones64 = self.cst[:, 128:256]
            self.c_rot = self.cst[:, 256:384]
            self.c_eps = self.cst[:, 1024:1025]
            self.c_one = self.cst[:, 1025:1026]
            self.vec_c = self.vecs[:, 0:16].rearrange("p (k c) -> p k c", c=2)
            self.vec_normw = self.vecs[:, 16:64].rearrange("p (l j k) -> p l j k", l=2, j=3)
            self.vec_adab = self.vecs[:, 64:208].rearrange("p (l i k) -> p l i k", l=2, i=9)
            self.vec_mlnw = self.vecs[:, 208:212].rearrange("p (l c) -> p l c", l=2)
            self.vec_qkw = self.vecs[:, 212:220].rearrange("p (l c) -> p l c", l=2)
            self.vec_gateb = self.vecs[:, 220:252].rearrange("p (l c) -> p l c", l=2)
            self.modtok = S.tok()
            t0 = S.tok()
            S.dma("sp", self.cst[:], self.consts_d, writes=[t0])
            S.dma("sp", self.vecs[:], self.vecs_d, writes=[t0])
            S.flush()
            self.run_phases()
        return nc

    def run_phases(self):
        LY = self.layers
        self.phase_setup()
        if self.done("biasprep"):
            return
        for l in range(LY):
            lastl = l == LY - 1
            self.modraw, self.modA, self.modG = self.modraw_all[:, l], self.modA_all[:, l], self.modG_all[:, l]
            self.phase_ffn(l, 0, self.xin if l == 0 else self.XT, self.XT)
            if self.done(f"ffn{l}0"):
                return
            self.phase_inproj(l)
            if self.done(f"inproj{l}"):
                return
            self.phase_mixers(l, not lastl)
            if self.done(f"attn{l}"):
                return
            self.phase_merge(l, not lastl)
            if self.done(f"merge{l}"):
                return
            self.phase_ffn(l, 1, self.XT, self.out if lastl else self.XT, last=lastl)


def host_consts():
    c = np.zeros((128, 1280), np.float32)
    c[:, 0:128] = 1.0 / D
    for b in range(2):
        c[b * 64:(b + 1) * 64, 128 + b * 64:128 + (b + 1) * 64] = 1.0 / 64
    for b in range(2):
        for m in range(64):
            if m < 32:
                c[b * 64 + m + 32, 256 + b * 64 + m] = -1.0
            else:
                c[b * 64 + m - 32, 256 + b * 64 + m] = 1.0
    c[:, 384:512] = np.eye(128, dtype=np.float32)
    li = np.arange(128)[:, None]
    ji = np.arange(128)[None, :]
    c[:, 512:640] = (li <= ji)
    c[:, 640:768] = (li >= ji)
    c[:, 768:896] = -c[:, 512:640]
    c[:, 896:1024] = -c[:, 640:768]
    c[:, 1024] = EPS
    c[:, 1025] = 1.0
    c[:, 1088:1152] = 1.0
    return c


def host_vecs(c_b, c_ctx, norm_w, ada_b, ml_norm_w, na_qk_w, gq_qk_w, ml_gate_b):
    v = np.zeros((128, 256), np.float32)
    cc = np.stack([c_b.reshape(8, 128).T, c_ctx.reshape(8, 128).T], axis=-1)
    v[:, 0:16] = cc.reshape(128, 16)
    v[:, 16:64] = norm_w.reshape(2, 3, 8, 128).transpose(3, 0, 1, 2).reshape(128, 48)
    v[:, 64:208] = ada_b.reshape(2, 9, 8, 128).transpose(3, 0, 1, 2).reshape(128, 144)
    v[:, 208:212] = ml_norm_w.reshape(2, 2, 128).transpose(2, 0, 1).reshape(128, 4)
    qk = np.stack([na_qk_w[:, 0], na_qk_w[:, 1], gq_qk_w[:, 0], gq_qk_w[:, 1]], axis=1)
    v[:, 212:220] = np.tile(qk, (1, 1, 2)).transpose(2, 0, 1).reshape(128, 8)
    v[:, 220:252] = np.broadcast_to(ml_gate_b.reshape(1, 32), (128, 32))
    return v


def host_rope(S):
    t = np.arange(S)
    row = (t // 64).astype(np.float32)
    col = (t % 64).astype(np.float32)
    nf = 16
    inv = (np.float32(10000.0) ** (-np.arange(nf, dtype=np.float32) / nf)).astype(np.float32)
    ang = np.concatenate([row[:, None] * inv, col[:, None] * inv], axis=-1).astype(np.float32)
    idx = np.arange(128) % 32
    r = np.zeros((2, 128, S), np.float32)
    r[0] = np.cos(ang)[:, idx].T
    r[1] = np.sin(ang)[:, idx].T
    return r


def host_rowmask(NR):
    NB = NR // 8
    m = np.zeros((3, 8, 128, 512), np.float32)
    for case, B in ((0, 0), (1, min(1, NB - 1)), (2, NB - 1)):
        kw = min(max(8 * B - 4, 0), NR - 16)
        for c in range(8):
            for a in range(2):
                kr = kw + 2 * c + a
                for j in range(8):
                    qr = 8 * B + j
                    rs = min(max(qr - 4, 0), NR - 8)
                    if not (rs <= kr < rs + 8):
                        m[case, c, a * 64:(a + 1) * 64, j * 64:(j + 1) * 64] = NEG
    return m


def host_rpbT(na_rpb):
    L = na_rpb.shape[0]
    kc = np.arange(64)[:, None]
    qc = np.arange(64)[None, :]
    cs = np.clip(qc - 8, 0, 48)
    valid = (kc >= cs) & (kc < cs + 16)
    off = np.clip(kc - qc + 15, 0, 30)
    out = np.full((L, 6, 31, 64, 64), NEG, np.float32)
    for r in range(31):
        ro = 22 - r
        if 0 <= ro <= 14:
            g = na_rpb[:, :, ro, :][:, :, off]
            out[:, :, r] = np.where(valid[None, None], g, np.float32(NEG))
    return out


def make_in_maps(x, c, ctx, c_ctx, ada_w, ada_b, norm_w, ffn_w_in, ffn_w_out, mix_w_in, ml_gate_b, ml_norm_w, na_qk_w,
                 na_rpb, gq_qk_w, w_br_ml, w_br_na, w_br_gq, w_out):
    f = lambda a: np.ascontiguousarray(np.asarray(a, dtype=np.float32))
    x, c, ctx, c_ctx = f(x), f(c), f(ctx), f(c_ctx)
    B, S, _ = x.shape
    consts = host_consts()
    rope = host_rope(S)
    rowmask = host_rowmask(S // 64)
    rpbT = host_rpbT(f(na_rpb))
    shared = {"ada_w": f(ada_w), "ffn_w_in": f(ffn_w_in), "ffn_w_out": f(ffn_w_out), "mix_w_in": f(mix_w_in),
              "w_br_ml": f(w_br_ml), "w_br_na": f(w_br_na), "w_br_gq": f(w_br_gq), "w_out": f(w_out),
              "consts": consts, "rope": rope, "rowmask": rowmask, "rpbT": rpbT}
    in_maps = []
    for b in range(B):
        m = dict(shared)
        m["xin"] = np.ascontiguousarray(np.concatenate([ctx[b].T, x[b].T], axis=1))
        m["vecs"] = host_vecs(c[b], c_ctx, f(norm_w), f(ada_b), f(ml_norm_w), f(na_qk_w), f(gq_qk_w), f(ml_gate_b))
        in_maps.append(m)
    return in_maps


def kernel(**inputs):
    x = np.asarray(inputs["x"])
    B, S, _ = x.shape
    kb = K(S=S)
    nc = kb.build()
    in_maps = make_in_maps(**inputs)
    res = run_bass_kernel_spmd(nc, in_maps, core_ids=list(range(B)))
    out = np.stack([np.ascontiguousarray(r["out"].T) for r in res.results], axis=0)
    return out.astype(np.float32)
```

```python
import numpy as np
from contextlib import ExitStack
import concourse.bass as bass
import concourse.mybir as mybir
from concourse.bass_utils import run_bass_kernel_spmd

F32, BF16 = mybir.dt.float32, mybir.dt.bfloat16
AF = mybir.ActivationFunctionType
ALU = mybir.AluOpType

D = 1024
L_CTX = 256
DFF = 2816
NIN = 5904
EPS = 1e-6
NEG = -30000.0


class Tok:
    __slots__ = ("w", "r")

    def __init__(self):
        self.w = None
        self.r = []


class Op:
    __slots__ = ("eng", "fn", "deps", "sig", "sem", "key", "val", "dma")


class Sched:
    ENG = ("pe", "act", "dve", "pool", "sp")

    def __init__(self, nc, es, ndma=16):
        self.nc = nc
        self.e = {"pe": nc.tensor, "act": nc.scalar, "dve": nc.vector, "pool": nc.gpsimd, "sp": nc.sync}
        self.sem = {k: es.enter_context(nc.semaphore("s_" + k)) for k in self.ENG}
        self.cnt = {k: 0 for k in self.ENG}
        self.bars = [es.enter_context(nc.semaphore(f"s_bar{i}")) for i in range(10)]
        self.barcnt = 0
        self.dq = ("sp", "pool", "act")
        self.dsem = {q: [es.enter_context(nc.semaphore(f"d_{q}{i}")) for i in range(ndma)] for q in self.dq}
        self.dval = {q: [0] * ndma for q in self.dq}
        self.dnext = {q: 0 for q in self.dq}
        self.known = {k: {} for k in self.ENG}
        self.ops = []
        self.toks = []
        self.nins = 0
        self.nsig = {k: 0 for k in self.ENG}
        self.nflush = 0

    def tok(self):
        t = Tok()
        self.toks.append(t)
        return t

    def add(self, eng, fn, reads=(), writes=(), dma=False):
        op = Op()
        op.eng, op.fn, op.dma, op.sig, op.sem, op.key, op.val = eng, fn, dma, dma, None, None, 0
        deps = []
        for t in reads:
            if t.w is not None:
                deps.append(t.w)
        for t in writes:
            if t.w is not None:
                deps.append(t.w)
            deps.extend(t.r)
        seen = set()
        d2 = []
        for d in deps:
            if id(d) in seen:
                continue
            seen.add(id(d))
            if d.eng == "pe" and eng == "pe" and not d.dma and not dma:
                continue
            if not d.sig:
                d.sig = True
                self.nsig[d.eng] += 1
            d2.append(d)
        op.deps = d2
        for t in reads:
            if not dma:
                t.r = [o for o in t.r if o.dma or o.eng != eng]
            t.r.append(op)
        for t in writes:
            t.w = op
            t.r = []
        self.ops.append(op)
        return op

    def mm(self, out, lhsT, rhs, start, stop, reads=(), writes=()):
        return self.add("pe", lambda e: e.matmul(out, lhsT, rhs, start=start, stop=stop), reads, writes)

    def act(self, out, in_, func, reads=(), writes=(), **kw):
        return self.add("act", lambda e: e.activation(out=out, in_=in_, func=func, **kw), reads, writes)

    def tt(self, eng, out, in0, in1, op, reads=(), writes=()):
        return self.add(eng, lambda e: e.tensor_tensor(out=out, in0=in0, in1=in1, op=op), reads, writes)

    def ts(self, eng, out, in0, s1, s2, op0, op1=None, reads=(), writes=()):
        if op1 is None:
            return self.add(eng, lambda e: e.tensor_scalar(out=out, in0=in0, scalar1=s1, scalar2=None, op0=op0), reads, writes)
        return self.add(eng, lambda e: e.tensor_scalar(out=out, in0=in0, scalar1=s1, scalar2=s2, op0=op0, op1=op1), reads, writes)

    def stt(self, eng, out, in0, scalar, in1, op0, op1, reads=(), writes=()):
        return self.add(eng, lambda e: e.scalar_tensor_tensor(out=out, in0=in0, scalar=scalar, in1=in1, op0=op0, op1=op1), reads, writes)

    def copy(self, eng, out, in_, reads=(), writes=()):
        if eng == "act":
            return self.add("act", lambda e: e.activation(out=out, in_=in_, func=AF.Copy), reads, writes)
        return self.add(eng, lambda e: e.tensor_copy(out=out, in_=in_), reads, writes)

    def recip(self, out, in_, reads=(), writes=()):
        return self.add("dve", lambda e: e.reciprocal(out=out, in_=in_), reads, writes)

    def memset(self, eng, ap, val, writes=()):
        return self.add(eng, lambda e: e.memset(ap, val), (), writes)

    def dma(self, q, out, in_, reads=(), writes=()):
        return self.add(q, lambda e: e.dma_start(out=out, in_=in_), reads, writes, dma=True)

    def _wait(self, eng, sem, key, val):
        kn = self.known[eng]
        if kn.get(key, 0) < val:
            self.e[eng].wait_ge(sem, val)
            kn[key] = val
            self.nins += 1

    def flush(self):
        last = {}
        for op in self.ops:
            if not op.dma:
                last[op.eng] = op
        for op in last.values():
            op.sig = True
        for op in self.ops:
            E = self.e[op.eng]
            for d in op.deps:
                self._wait(op.eng, d.sem, d.key, d.val)
            if op.dma:
                q = op.eng
                i = self.dnext[q]
                self.dnext[q] = (i + 1) % len(self.dsem[q])
                s = self.dsem[q][i]
                key = f"d_{q}{i}"
                pv = self.dval[q][i]
                if pv > 0:
                    self._wait(q, s, key, pv)
                ins = op.fn(E)
                ins.then_inc(s, 16)
                self.dval[q][i] = pv + 16
                op.sem, op.key, op.val = s, key, pv + 16
            else:
                ins = op.fn(E)
                if op.sig:
                    self.cnt[op.eng] += 1
                    op.sem, op.key, op.val = self.sem[op.eng], "s_" + op.eng, self.cnt[op.eng]
                    ins.then_inc(op.sem, 1)
            self.nins += 1
        for k in self.ENG:
            if k != "sp" and self.cnt[k] > 0:
                self._wait("sp", self.sem[k], "s_" + k, self.cnt[k])
        if self.cnt["sp"] > 0:
            self._wait("sp", self.sem["sp"], "s_sp", self.cnt["sp"])
        for q in self.dq:
            for i, s in enumerate(self.dsem[q]):
                if self.dval[q][i] > 0:
                    self._wait("sp", s, f"d_{q}{i}", self.dval[q][i])
        bsem = self.bars[self.barcnt // 10]
        bval = self.barcnt % 10 + 1
        self.barcnt += 1
        self.e["sp"].sem_inc(bsem, 1)
        for k in self.ENG:
            if k != "sp":
                self.e[k].wait_ge(bsem, bval)
        for k in self.ENG:
            kn = self.known[k]
            for k2 in self.ENG:
                kn["s_" + k2] = self.cnt[k2]
            for q in self.dq:
                for i in range(len(self.dsem[q])):
                    kn[f"d_{q}{i}"] = self.dval[q][i]
        self.nsig = {k: 0 for k in self.ENG}
        self.nflush += 1
        for t in self.toks:
            t.w = None
            t.r = []
        self.ops = []


def _maybe_flush(self, limit=100000):
    if max(self.nsig.values()) > limit:
        self.flush()


Sched.maybe_flush = _maybe_flush


class Ring:
    def __init__(self, K, name, n, shape, dtype, psum=False):
        self.bufs = []
        for i in range(n):
            if psum:
                t = K.ps(name, shape, dtype)
            else:
                t = K.sb(name, shape, dtype)
            self.bufs.append((t, K.S.tok()))
        self.i = 0

    def next(self):
        b = self.bufs[self.i]
        self.i = (self.i + 1) % len(self.bufs)
        return b


class K:
    def __init__(self, S=4096, layers=2, debug=False, stop_after=None):
        self.S_lat = S
        self.T = L_CTX + S
        self.NR = S // 64
        self.NB = self.NR // 8
        self.NCH = self.T // 128
        self.layers = layers
        self.debug = debug
        self.stop_after = stop_after
        self.nc = bass.Bass("TRN2", target_bir_lowering=False)
        self.dbg_names = []

    def din(self, name, shape, dt=F32):
        return self.nc.dram_tensor(name, list(shape), dt, kind="ExternalInput").ap()

    def dscr(self, name, shape, dt=F32):
        if self.debug:
            self.dbg_names.append(name)
            return self.nc.dram_tensor(name, list(shape), dt, kind="ExternalOutput").ap()
        return self.nc.dram_tensor(name, list(shape), dt).ap()

    def uq(self, name):
        self._uq = getattr(self, "_uq", 0) + 1
        return f"{name}_{self._uq}"

    def sb(self, name, shape, dt=F32):
        return self.es.enter_context(self.nc.sbuf_tensor(self.uq(name), list(shape), dt))

    def ps(self, name, shape, dt=F32):
        return self.es.enter_context(self.nc.psum_tensor(self.uq(name), list(shape), dt))

    def tiles(self, n=512):
        out = [(0, L_CTX, True)] if n >= L_CTX else [(i * n, n, True) for i in range(L_CTX // n)]
        out += [(L_CTX + n * i, n, False) for i in range(self.S_lat // n)]
        return out

    def done(self, name):
        return self.stop_after == name

    def load_w(self, dst, src, kc_n, ncols, stage, colblk, toks=None, blk_order=None):
        S = self.S
        srcv = src.rearrange("(kc p) n -> p kc n", p=128)
        engs = ("act", "dve", "pool", "dve", "act")
        nblk = (ncols + colblk - 1) // colblk
        if toks is None:
            seq = [(kc, b) for kc in range(kc_n) for b in range(nblk)]
        else:
            order = blk_order if blk_order is not None else list(range(nblk))
            seq = [(kc, b) for b in order for kc in range(kc_n)]
        for i, (kc, b) in enumerate(seq):
            c0 = b * colblk
            cw = min(colblk, ncols - c0)
            st, stok = stage.next()
            S.dma(("sp", "pool", "act")[i % 3], st[:, 0:cw], srcv[:, kc, c0:c0 + cw], writes=[stok])
            S.copy(engs[i % 5], dst[:, kc, c0:c0 + cw], st[:, 0:cw], reads=[stok], writes=[self.wtok if toks is None else toks[(kc, b)]])

    @staticmethod
    def blk_toks(toks, colblk, kc, c0, cw):
        return [toks[(kc, b)] for b in range(c0 // colblk, (c0 + cw - 1) // colblk + 1)]

    def modulate(self, xt, xtok, n, j, c, h, htok, sq_ring, ps_ms, ps_ms_tok, sd, sdtok):
        S = self.S
        for kc in range(8):
            sq, sqtok = sq_ring.next()
            S.act(sq[:, 0:n], xt[:, kc, 0:n], AF.Square, reads=[xtok], writes=[sqtok])
            S.mm(ps_ms[:, 0:n], self.c_onesD, sq[:, 0:n], kc == 0, kc == 7, reads=[sqtok], writes=[ps_ms_tok])
        S.act(sd[:, 0:n], ps_ms[:, 0:n], AF.Sqrt, reads=[ps_ms_tok], writes=[sdtok], bias=self.c_eps, scale=1.0)
        S.recip(sd[:, 0:n], sd[:, 0:n], reads=[sdtok], writes=[sdtok])
        for kc in range(8):
            tmp, ttok = sq_ring.next()
            S.stt("dve", tmp[:, 0:n], xt[:, kc, 0:n], self.modA[:, j, c, kc:kc + 1], sd[:, 0:n], ALU.mult, ALU.mult,
                  reads=[xtok, sdtok, self.modtok], writes=[ttok])
            S.act(h[:, kc, 0:n], tmp[:, 0:n], AF.Identity, reads=[ttok, self.modtok], writes=[htok],
                  bias=self.modraw[:, 3 * j, kc, c:c + 1], scale=1.0)

    def phase_adaln(self, l):
        S, nc = self.S, self.nc
        with ExitStack() as es:
            self.es = es
            stage = Ring(self, "adst", 2, [128, 8, 1024], F32)
            sc = self.sb("ad_sc", [128, 8, 2])
            sctok = S.tok()
            pst = self.ps("ad_ps", [128, 8, 2])
            pstok = S.tok()
            S.act(sc[:], self.vec_c[:], AF.Silu, writes=[sctok])
            awv = self.ada_w[l].rearrange("(kc p) n -> p kc n", p=128)
            for i in range(9):
                st, stok = stage.next()
                S.dma("sp", st[:, 0:4, :], awv[:, 0:4, i * 1024:(i + 1) * 1024], writes=[stok])
                S.dma("pool", st[:, 4:8, :], awv[:, 4:8, i * 1024:(i + 1) * 1024], writes=[stok])
                for oc in range(8):
                    for kc in range(8):
                        S.mm(pst[:, oc, :], st[:, kc, oc * 128:(oc + 1) * 128], sc[:, kc, :], kc == 0, kc == 7,
                             reads=[stok, sctok], writes=[pstok])
                for c in range(2):
                    S.tt("dve", self.modraw[:, i, :, c], pst[:, :, c], self.vec_adab[:, l, i, :], ALU.add,
                         reads=[pstok], writes=[self.modtok])
                S.maybe_flush()
            for j in range(3):
                for c in range(2):
                    S.stt("dve", self.modA[:, j, c, :], self.modraw[:, 3 * j + 1, :, c], 1.0, self.vec_normw[:, l, j, :],
                          ALU.add, ALU.mult, reads=[self.modtok], writes=[self.modtok])
                    S.ts("dve", self.modG[:, j, c, :], self.modraw[:, 3 * j + 2, :, c], 0.5 if j != 1 else 1.0, None, ALU.mult,
                         reads=[self.modtok], writes=[self.modtok])
            S.flush()

    def phase_ffn(self, l, f, src, dst, last=False):
        S, nc = self.S, self.nc
        j = 0 if f == 0 else 2
        N = 256
        with ExitStack() as es:
            self.es = es
            win = self.sb("f_win", [128, 8, 2 * DFF], BF16)
            wout = self.sb("f_wout", [128, 22, D], BF16)
            self.wtok = S.tok()
            stage = Ring(self, "f_st", 4, [128, 704], F32)
            xr = Ring(self, "f_x", 3, [128, 8, N], F32)
            sqr = Ring(self, "f_sq", 3, [128, N], F32)
            sd = self.sb("f_sd", [128, N])
            sdtok = S.tok()
            hr = Ring(self, "f_h", 2, [128, 8, N], BF16)
            ar = Ring(self, "f_a", 1, [128, 22, N], BF16)
            sgr = Ring(self, "f_sg", 2, [128, N], F32)
            ps_ms = self.ps("f_pms", [128, 512])
            ps_ms_tok = S.tok()
            pg = Ring(self, "f_pg", 2, [128, 512], F32, psum=True)
            pu = Ring(self, "f_pu", 2, [128, 512], F32, psum=True)
            po = Ring(self, "f_po", 2, [128, 512], F32, psum=True)
            win_toks = {(kc, b): S.tok() for kc in range(8) for b in range(8)}
            wout_toks = {(kc, b): S.tok() for kc in range(22) for b in range(2)}
            self.load_w(win, self.ffn_w_in[l, f], 8, 2 * DFF, stage, 704, toks=win_toks, blk_order=[0, 4, 1, 5, 2, 6, 3, 7])
            self.load_w(wout, self.ffn_w_out[l, f], 22, D, stage, 512, toks=wout_toks)
            gtoks = lambda m, kc: self.blk_toks(win_toks, 704, kc, m * 128, 128)
            utoks = lambda m, kc: self.blk_toks(win_toks, 704, kc, DFF + m * 128, 128)
            tl = [t for t in self.tiles(N) if not (last and t[2])]

            def prep(ti):
                t0, n, isctx = tl[ti]
                xt, xtok = xr.next()
                S.dma("sp", xt[:, :, 0:n], src[:, t0:t0 + n].rearrange("(kc p) t -> p kc t", p=128), writes=[xtok])
                h, htok = hr.next()
                self.modulate(xt, xtok, n, j, 1 if isctx else 0, h, htok, sqr, ps_ms, ps_ms_tok, sd, sdtok)
                return xt, xtok, h, htok

            nxt = prep(0)
            for ti, (t0, n, isctx) in enumerate(tl):
                c = 1 if isctx else 0
                xt, xtok, h, htok = nxt
                a, atok = ar.next()
                for m in range(22):
                    g, gtok = pg.next()
                    u, utok = pu.next()
                    for kc in range(8):
                        S.mm(g[:, 0:n], win[:, kc, m * 128:(m + 1) * 128], h[:, kc, 0:n], kc == 0, kc == 7,
                             reads=gtoks(m, kc) + [htok], writes=[gtok])
                    for kc in range(8):
                        S.mm(u[:, 0:n], win[:, kc, DFF + m * 128:DFF + (m + 1) * 128], h[:, kc, 0:n], kc == 0, kc == 7,
                             reads=utoks(m, kc) + [htok], writes=[utok])
                    sg, sgtok = sgr.next()
                    S.act(sg[:, 0:n], g[:, 0:n], AF.Silu, reads=[gtok], writes=[sgtok])
                    S.tt("dve", a[:, m, 0:n], sg[:, 0:n], u[:, 0:n], ALU.mult, reads=[sgtok, utok], writes=[atok])
                    if m == 12 and ti + 1 < len(tl):
                        nxt = prep(ti + 1)
                for oc in range(8):
                    o, otok = po.next()
                    for m in range(22):
                        S.mm(o[:, 0:n], wout[:, m, oc * 128:(oc + 1) * 128], a[:, m, 0:n], m == 0, m == 21,
                             reads=[wout_toks[(m, oc // 4)], atok], writes=[otok])
                    S.stt("dve", xt[:, oc, 0:n], o[:, 0:n], self.modG[:, j, c, oc:oc + 1], xt[:, oc, 0:n], ALU.mult, ALU.add,
                          reads=[otok, self.modtok, xtok], writes=[xtok])
                if last:
                    S.dma("pool", dst[:, t0 - L_CTX:t0 - L_CTX + n].rearrange("(kc p) t -> p kc t", p=128), xt[:, :, 0:n], reads=[xtok])
                else:
                    S.dma("pool", dst[:, t0:t0 + n].rearrange("(kc p) t -> p kc t", p=128), xt[:, :, 0:n], reads=[xtok])
                S.maybe_flush()
            S.flush()


    def phase_inproj(self, l):
        S = self.S
        with ExitStack() as es:
            self.es = es
            win = self.sb("i_win", [128, 8, NIN], BF16)
            self.wtok = S.tok()
            wtok = self.wtok
            stage = Ring(self, "i_st", 4, [128, 1476], F32)
            xr = Ring(self, "i_x", 2, [128, 8, 512], F32)
            hr = Ring(self, "i_h", 2, [128, 8, 512], BF16)
            sqr = Ring(self, "i_sq", 2, [128, 512], F32)
            sd = self.sb("i_sd", [128, 512])
            sdtok = S.tok()
            o32 = Ring(self, "i_o32", 3, [128, 512], F32)
            o16 = Ring(self, "i_o16", 3, [128, 512], BF16)
            qnr = Ring(self, "i_qn", 3, [128, 512], F32)
            rsr = Ring(self, "i_rs", 3, [128, 512], F32)
            csr = Ring(self, "i_cs", 2, [128, 2, 512], F32)
            ps_ms = self.ps("i_pms", [128, 512])
            ps_ms_tok = S.tok()
            pmain = Ring(self, "i_pm", 4, [128, 512], F32, psum=True)
            pstat = Ring(self, "i_pst", 2, [128, 512], F32, psum=True)
            prot = Ring(self, "i_prot", 1, [128, 512], F32, psum=True)
            win_toks = {(kc, b): S.tok() for kc in range(8) for b in range(4)}
            self.load_w(win, self.mix_w_in[l], 8, NIN, stage, 1476, toks=win_toks)
            wt = lambda kc, c0, cw: self.blk_toks(win_toks, 1476, kc, c0, cw)
            FM = []
            for i in range(2):
                FM.append((128 * i, "raw", self.MLKT, 128 * i, 0))
            for i in range(3):
                FM.append((528 + 128 * i, "norm", self.NAKT, 128 * i, 1))
            FM.append((1296, "rope", self.GQKT, 0, 3))
            for i in range(2):
                FM.append((1552 + 128 * i, "raw", self.MLQT, 128 * i, 0))
            for i in range(2):
                FM.append((1808 + 128 * i, "sig", self.SIGOT, 128 * i, 0))
            for i in range(3):
                FM.append((2064 + 128 * i, "norm", self.NAQT, 128 * i, 0))
            for i in range(3):
                FM.append((2448 + 128 * i, "rope", self.GQQT, 128 * i, 2))
            for i in range(24):
                FM.append((2832 + 128 * i, "sig", self.BRGT, 128 * i, 0))
            tl = self.tiles(512)
            self._dq = 0

            def nextq():
                self._dq += 1
                return "sp" if self._dq % 2 == 0 else "pool"

            def prep(ti):
                t0, n, isctx = tl[ti]
                xt, xtok = xr.next()
                S.dma("sp", xt[:, :, 0:n], self.XT[:, t0:t0 + n].rearrange("(kc p) t -> p kc t", p=128), writes=[xtok])
                h, htok = hr.next()
                self.modulate(xt, xtok, n, 1, 1 if isctx else 0, h, htok, sqr, ps_ms, ps_ms_tok, sd, sdtok)
                cs, cstok = (None, None)
                if not isctx:
                    cs, cstok = csr.next()
                    S.dma("pool", cs[:, :, 0:n], self.rope_d[:, :, t0 - L_CTX:t0 - L_CTX + n].rearrange("c p t -> p c t"), writes=[cstok])
                return h, htok, cs, cstok

            def epilogue(kind, dsl, widx, p, ptok, n, isctx, cs, cstok):
                q = nextq()
                if kind == "raw" or kind == "sig":
                    o, otok = o32.next()
                    S.act(o[:, 0:n], p[:, 0:n], AF.Copy if kind == "raw" else AF.Sigmoid, reads=[ptok], writes=[otok])
                    S.dma(q, dsl, o[:, 0:n], reads=[otok])
                    return
                sq, sqtok = rsr.next()
                S.act(sq[:, 0:n], p[:, 0:n], AF.Square, reads=[ptok], writes=[sqtok])
                qn, qntok = qnr.next()
                S.act(qn[:, 0:n], p[:, 0:n], AF.Copy, reads=[ptok], writes=[qntok])
                pst, psttok = pstat.next()
                S.mm(pst[:, 0:n], self.c_ones64, sq[:, 0:n], True, True, reads=[sqtok], writes=[psttok])
                rs, rstok = rsr.next()
                S.act(rs[:, 0:n], pst[:, 0:n], AF.Sqrt, reads=[psttok], writes=[rstok], bias=self.c_eps, scale=1.0)
                S.recip(rs[:, 0:n], rs[:, 0:n], reads=[rstok], writes=[rstok])
                wq = self.vec_qkw[:, l, widx:widx + 1]
                if kind == "norm" or isctx:
                    o, otok = o16.next()
                    S.stt("dve", o[:, 0:n], qn[:, 0:n], wq, rs[:, 0:n], ALU.mult, ALU.mult, reads=[qntok, rstok], writes=[otok])
                    S.dma(q, dsl, o[:, 0:n], reads=[otok])
                    return
                S.stt("dve", qn[:, 0:n], qn[:, 0:n], wq, rs[:, 0:n], ALU.mult, ALU.mult, reads=[qntok, rstok], writes=[qntok])
                pr, prtok = prot.next()
                S.mm(pr[:, 0:n], self.c_rot, qn[:, 0:n], True, True, reads=[qntok], writes=[prtok])
                t1, t1tok = o32.next()
                S.tt("dve", t1[:, 0:n], qn[:, 0:n], cs[:, 0, 0:n], ALU.mult, reads=[qntok, cstok], writes=[t1tok])
                S.tt("dve", qn[:, 0:n], pr[:, 0:n], cs[:, 1, 0:n], ALU.mult, reads=[prtok, cstok], writes=[qntok])
                o, otok = o16.next()
                S.tt("dve", o[:, 0:n], t1[:, 0:n], qn[:, 0:n], ALU.add, reads=[t1tok, qntok], writes=[otok])
                S.dma(q, dsl, o[:, 0:n], reads=[otok])

            def tm_epilogue(p, ptok, cw, dst, is16, tt0):
                o, otok = (o16 if is16 else o32).next()
                S.copy("dve", o[:, 0:cw], p[:, 0:cw], reads=[ptok], writes=[otok])
                S.dma(nextq(), dst[tt0:tt0 + 128, 0:cw], o[:, 0:cw], reads=[otok])

            nxt = prep(0)
            for ti, (t0, n, isctx) in enumerate(tl):
                h, htok, cs, cstok = nxt
                pend = None
                for idx, (col0, kind, dst, row0, widx) in enumerate(FM):
                    p, ptok = pmain.next()
                    for kc in range(8):
                        S.mm(p[:, 0:n], win[:, kc, col0:col0 + 128], h[:, kc, 0:n], kc == 0, kc == 7, reads=wt(kc, col0, 128) + [htok], writes=[ptok])
                    if pend is not None:
                        epilogue(*pend)
                    pend = (kind, dst[row0:row0 + 128, t0:t0 + n], widx, p, ptok, n, isctx, cs, cstok)
                    if idx == 24 and ti + 1 < len(tl):
                        nxt = prep(ti + 1)
                tpend = None
                for s_ in range(n // 128):
                    tt0 = t0 + s_ * 128
                    for (c0, cw, dst, is16) in ((0, 512, self.MLKV, False), (512, 16, self.MLG, False),
                                                (912, 384, self.NAV, True), (1424, 128, self.GQV, True)):
                        p, ptok = pmain.next()
                        for kc in range(8):
                            S.mm(p[:, 0:cw], h[:, kc, s_ * 128:(s_ + 1) * 128], win[:, kc, c0:c0 + cw], kc == 0, kc == 7,
                                 reads=wt(kc, c0, cw) + [htok], writes=[ptok])
                        if pend is not None:
                            epilogue(*pend)
                            pend = None
                        if tpend is not None:
                            tm_epilogue(*tpend)
                        tpend = (p, ptok, cw, dst, is16, tt0)
                tm_epilogue(*tpend)
                S.maybe_flush()
            S.flush()

    def phase_biasprep(self):
        S = self.S
        with ExitStack() as es:
            self.es = es
            rm = self.sb("b_rm", [128, 3, 8, 512], F32)
            rmtok = S.tok()
            S.dma("sp", rm[:], self.rowmask_d.rearrange("c k p q -> p c k q"), writes=[rmtok])
            tr = Ring(self, "b_t", 4, [128, 8, 64], F32)
            i = 0
            for l in range(self.layers):
                for h in range(6):
                    for case, delta in ((0, 0), (1, -4), (2, -8)):
                        for c in range(8):
                            t, ttok = tr.next()
                            for a in range(2):
                                r0 = 15 - 2 * c - a - delta
                                S.dma("sp" if i % 2 == 0 else "pool", t[a * 64:(a + 1) * 64, :, :],
                                      self.rpbT_d[l, h, r0:r0 + 8].rearrange("j k q -> k j q"), writes=[ttok])
                                i += 1
                            tv = t[:].rearrange("p j q -> p (j q)")
                            S.tt("pool", tv, tv, rm[:, case, c, :], ALU.add, reads=[ttok, rmtok], writes=[ttok])
                            S.act(tv, tv, AF.Identity, reads=[ttok], writes=[ttok], scale=8.0)
                            S.dma("act", self.BT[l, h, case, c], tv, reads=[ttok])
                        S.maybe_flush()
            S.flush()

    def phase_mlstm(self, l):
        S = self.S
        NCH = self.NCH
        with ExitStack() as es:
            self.es = es
            TRI = [self.cst[:, 512:640], self.cst[:, 640:768]]
            NTRI = [self.cst[:, 768:896], self.cst[:, 896:1024]]
            IDN = self.cst[:, 384:512]
            ones64f = self.cst[:, 1088:1152]
            gr = Ring(self, "m_g", 3, [128, 16], F32)
            kvr = Ring(self, "m_kv", 3, [128, 512], F32)
            qkr = Ring(self, "m_qk", 3, [128, 4, 128], F32)
            g2r = Ring(self, "m_g2", 3, [128, 8], F32)
            e1r = Ring(self, "m_e1", 3, [128, 8], F32)
            repr_ = Ring(self, "m_rep", 3, [128, 8, 64], F32)
            kpr = Ring(self, "m_kp", 3, [128, 256], F32)
            vbr = Ring(self, "m_vb", 3, [128, 2, 256], F32)
            ebr = Ring(self, "m_eb", 8, [128, 256], F32)
            qpr = Ring(self, "m_qp", 4, [128, 2, 128], F32)
            kptr = Ring(self, "m_kpt", 4, [128, 2, 128], F32)
            atr = Ring(self, "m_at", 4, [128, 128], F32)
            adr = Ring(self, "m_ad", 4, [64, 128], F32)
            hor = Ring(self, "m_ho", 4, [64, 128], F32)
            tmpr = Ring(self, "m_tmp", 3, [128, 256], F32)
            St = [[(self.sb("m_st", [128, 256]), S.tok()) for hp in range(2)] for d in range(2)]
            Stb = [[(self.sb("m_stb", [128, 256], F32), S.tok()) for hp in range(2)] for d in range(2)]
            pab = Ring(self, "m_pab", 2, [128, 256], F32, psum=True)
            psm = self.ps("m_psm", [128, 8])
            psmtok = S.tok()
            pSr = Ring(self, "m_pS", 2, [128, 128], F32, psum=True)
            pNr = Ring(self, "m_pN", 2, [128, 128], F32, psum=True)
            pUr = Ring(self, "m_pU", 1, [128, 256], F32, psum=True)
            for d in range(2):
                for hp in range(2):
                    S.memset("dve", St[d][hp][0][:], 0.0, writes=[St[d][hp][1]])
                    S.memset("dve", Stb[d][hp][0][:], 0.0, writes=[Stb[d][hp][1]])
            for (vb, vbtok) in vbr.bufs:
                vb4 = vb[:].rearrange("p a (h c) -> p a h c", h=2)
                S.memset("dve", vb4[:, :, :, 64:128], 1.0, writes=[vbtok])
            self._dq = 0
            order_f = list(range(NCH))
            order_b = [1, 0] + list(range(NCH - 1, 1, -1))
            dq = 0
            def unit(d, c):
                if True:
                    t0 = 128 * c
                    last = 127 if d == 0 else 0
                    HT = self.HFT if d == 0 else self.HBT
                    gt, gtok = gr.next()
                    S.dma("sp", gt[:], self.MLG[t0:t0 + 128, :], writes=[gtok])
                    kv, kvtok = kvr.next()
                    S.dma("pool", kv[:], self.MLKV[t0:t0 + 128, :], writes=[kvtok])
                    qk, qktok = qkr.next()
                    S.dma("sp", qk[:, 0:2, :], self.MLQT[:, t0:t0 + 128].rearrange("(c p) t -> p c t", p=128), writes=[qktok])
                    S.dma("pool", qk[:, 2:4, :], self.MLKT[:, t0:t0 + 128].rearrange("(c p) t -> p c t", p=128), writes=[qktok])
                    g2, g2tok = g2r.next()
                    S.tt("dve", g2[:], gt[:, 8 * d:8 * d + 8], self.vec_gateb[:, l, 8 * d:8 * d + 8], ALU.add, reads=[gtok], writes=[g2tok])
                    e1, e1tok = e1r.next()
                    S.act(e1[:, 0:4], g2[:, 4:8], AF.Exp, reads=[g2tok], writes=[e1tok], scale=-1.0)
                    S.act(e1[:, 4:8], e1[:, 0:4], AF.Ln, reads=[e1tok], writes=[e1tok], bias=self.c_one, scale=1.0)
                    S.ts("dve", g2[:, 4:8], e1[:, 4:8], -1.0, None, ALU.mult, reads=[e1tok], writes=[g2tok])
                    yield
                    rep, reptok = repr_.next()
                    for i in range(8):
                        if i % 2 == 0:
                            S.ts("dve", rep[:, i, :], ones64f, g2[:, i:i + 1], None, ALU.mult, reads=[g2tok], writes=[reptok])
                        else:
                            S.act(rep[:, i, :], ones64f, AF.Identity, reads=[g2tok], writes=[reptok], scale=g2[:, i:i + 1])
                    yield
                    S.mm(psm[:, 0:4], TRI[d], g2[:, 4:8], True, True, reads=[g2tok], writes=[psmtok])
                    S.tt("dve", e1[:, 0:4], g2[:, 0:4], psm[:, 0:4], ALU.subtract, reads=[g2tok, psmtok], writes=[e1tok])
                    S.act(e1[:, 0:4], e1[:, 0:4], AF.Exp, reads=[e1tok], writes=[e1tok])
                    kp, kptok = kpr.next()
                    for h in range(4):
                        S.ts("dve", kp[:, h * 64:(h + 1) * 64], kv[:, h * 64:(h + 1) * 64], e1[:, h:h + 1], 0.125, ALU.mult, ALU.mult,
                             reads=[kvtok, e1tok], writes=[kptok])
                    yield
                    vb, vbtok = vbr.next()
                    S.copy("act", vb[:].rearrange("p a (h c) -> p a h c", h=2)[:, :, :, 0:64],
                           kv[:, 256:512].rearrange("p (a h c) -> p a h c", a=2, h=2), reads=[kvtok], writes=[vbtok])
                    qp, qptok = qpr.next()
                    kpt, kpttok = kptr.next()
                    for hp in range(2):
                        lhs_lf = rep[:, 4 + 2 * hp:6 + 2 * hp, :].rearrange("p a b -> p (a b)")
                        lhs_ig = rep[:, 2 * hp:2 * hp + 2, :].rearrange("p a b -> p (a b)")
                        pa, patok = pab.next()
                        S.mm(pa[:, 0:128], lhs_lf, TRI[d], True, True, reads=[reptok], writes=[patok])
                        S.mm(pa[:, 128:256], lhs_ig, IDN, True, False, reads=[reptok], writes=[patok])
                        S.mm(pa[:, 128:256], lhs_lf, NTRI[d], False, True, reads=[reptok], writes=[patok])
                        yield
                        eb, ebtok = ebr.next()
                        S.act(eb[:], pa[:], AF.Exp, reads=[patok], writes=[ebtok])
                        S.tt("dve", qp[:, hp, :], qk[:, hp, :], eb[:, 0:128], ALU.mult, reads=[qktok, ebtok], writes=[qptok])
                        S.stt("dve", kpt[:, hp, :], qk[:, 2 + hp, :], 0.125, eb[:, 128:256], ALU.mult, ALU.mult, reads=[qktok, ebtok], writes=[kpttok])
                        ebt = eb[:, last:last + 1]
                        yield
                        st, sttok = St[d][hp]
                        stb, stbtok = Stb[d][hp]
                        for hh in range(2):
                            h = 2 * hp + hh
                            hb = 64 * hh
                            pS, pStok = pSr.next()
                            S.mm(pS[:], kpt[hb:hb + 64, hp, :], qp[hb:hb + 64, hp, :], True, True, reads=[kpttok, qptok], writes=[pStok])
                            yield
                            at, attok = atr.next()
                            S.tt("dve", at[:], pS[:], TRI[d], ALU.mult, reads=[pStok], writes=[attok])
                            pN, pNtok = pNr.next()
                            S.mm(pN[:, :], vb[:, hp, hh * 128:(hh + 1) * 128], at[:], True, False, reads=[vbtok, attok], writes=[pNtok])
                            S.mm(pN[:, :], stb[hb:hb + 64, hh * 128:(hh + 1) * 128], qp[hb:hb + 64, hp, :], False, True, reads=[stbtok, qptok], writes=[pNtok])
                            yield
                            ad, adtok = adr.next()
                            S.act(ad[:], pN[64:128, :], AF.Abs, reads=[pNtok], writes=[adtok])
                            S.ts("dve", ad[:], ad[:], 1.0, None, ALU.max, reads=[adtok], writes=[adtok])
                            S.recip(ad[:], ad[:], reads=[adtok], writes=[adtok])
                            ho, hotok = hor.next()
                            S.tt("dve", ho[:], pN[0:64, :], ad[:], ALU.mult, reads=[pNtok, adtok], writes=[hotok])
                            S.dma("sp" if self._dq % 2 == 0 else "pool", HT[h * 64:(h + 1) * 64, t0:t0 + 128], ho[:], reads=[hotok])
                            self._dq += 1
                        pU, pUtok = pUr.next()
                        S.mm(pU[:], kp[:, hp * 128:(hp + 1) * 128], vb[:, hp, :], True, True, reads=[kptok, vbtok], writes=[pUtok])
                        tmp, tmptok = tmpr.next()
                        S.tt("dve", tmp[:], st[:], pU[:], ALU.add, reads=[sttok, pUtok], writes=[tmptok])
                        S.ts("dve", st[:], tmp[:], ebt, None, ALU.mult, reads=[tmptok, ebtok], writes=[sttok])
                        S.act(stb[:], tmp[:], AF.Identity, reads=[tmptok, ebtok], writes=[stbtok], scale=ebt)
                        yield

            for step in range(NCH):
                gens = [unit(0, order_f[step]), unit(1, order_b[step])]
                while gens:
                    for g_ in list(gens):
                        try:
                            next(g_)
                        except StopIteration:
                            gens.remove(g_)
                S.maybe_flush()
            S.flush()

    def attn_res(self):
        S = self.S
        NCH, T = self.NCH, self.T
        kt_g = self.sb("a_ktg", [128, 2, T], BF16)
        kt_n = self.sb("a_ktn", [128, 3, T], BF16)
        vg = self.sb("a_vg", [128, NCH, 2, 128], BF16)
        vn = self.sb("a_vn", [128, NCH, 6, 128], BF16)
        rtok = S.tok()
        S.memset("dve", vg[:, :, :, 64:128], 1.0, writes=[rtok])
        S.memset("pool", vn[:, :, :, 64:128], 1.0, writes=[rtok])
        for kv in range(2):
            for half in range(2):
                S.dma("sp", kt_g[half * 64:(half + 1) * 64, kv, :], self.GQKT[kv * 64:(kv + 1) * 64, :], writes=[rtok])
        for i in range(3):
            S.dma("sp", kt_n[:, i, :], self.NAKT[i * 128:(i + 1) * 128, :], writes=[rtok])
        gqv = self.GQV.rearrange("(c p) (h d) -> p c h d", p=128, d=64)
        nav = self.NAV.rearrange("(c p) (h d) -> p c h d", p=128, d=64)
        for h in range(2):
            S.dma("pool", vg[:, :, h, 0:64], gqv[:, :, h, :], writes=[rtok])
        for h in range(6):
            S.dma("pool" if h % 2 == 0 else "act", vn[:, :, h, 0:64], nav[:, :, h, :], writes=[rtok])
        return kt_g, kt_n, vg, vn, rtok

    def phase_mixers(self, l, ctx_out):
        with ExitStack() as outer:
            self.es = outer
            res = self.attn_res()
            self.phase_mlstm(l)
            self.phase_attn(l, ctx_out, res)

    def phase_attn(self, l, ctx_out, res):
        S = self.S
        NCH, T, NB, NR = self.NCH, self.T, self.NB, self.NR
        with ExitStack() as es:
            self.es = es
            kt_g, kt_n, vg, vn, rtok = res
            qrs = [Ring(self, "a_q", 2, [128, 512], BF16) for _ in range(2)]
            for par in range(2):
                for (qb, qbtok) in qrs[par].bufs:
                    S.memset("dve", qb[:], 0.0, writes=[qbtok])
            btr = Ring(self, "a_bt", 2, [128, 8, 512], F32)
            G = 3
            ptr = Ring(self, "a_pt", 3, [128, G, 512], BF16)
            rdr = Ring(self, "a_rd", 2, [64, 512], F32)
            obr = Ring(self, "a_ob", 2, [64, 512], BF16)
            pSr = Ring(self, "a_pS", 2, [128, G, 512], F32, psum=True)
            pNr = Ring(self, "a_pN", 2, [128, 512], F32, psum=True)
            self._dq = 0

            def attend(qsrc, h, hb, tq0, n, kt, ktc, keychunks, vfn, dst):
                q, qtok = qrs[hb // 64].next()
                S.dma("sp", q[hb:hb + 64, 0:n], qsrc[h * 64:(h + 1) * 64, tq0:tq0 + n], writes=[qtok])
                pn, pntok = pNr.next()
                nk = len(keychunks)
                groups = [list(range(i, min(i + G, nk))) for i in range(0, nk, G)]
                pst = [None] * len(groups)

                def qk(gi):
                    ps_, pstok = pSr.next()
                    for jj, i in enumerate(groups[gi]):
                        kc, bias, btok = keychunks[i]
                        S.mm(ps_[:, jj, 0:n], kt[:, ktc, kc * 128:(kc + 1) * 128], q[:, 0:n], True, bias is None,
                             reads=[rtok, qtok], writes=[pstok])
                        if bias is not None:
                            S.mm(ps_[:, jj, 0:n], self.cst[:, 384:512], bias[:, 0:n], False, True, reads=[btok], writes=[pstok])
                    pst[gi] = (ps_, pstok)

                qk(0)
                for gi, grp in enumerate(groups):
                    if gi + 1 < len(groups):
                        qk(gi + 1)
                    ps_, pstok = pst[gi]
                    pt, pttok = ptr.next()
                    S.act(pt[:, 0:len(grp), 0:n], ps_[:, 0:len(grp), 0:n], AF.Exp, reads=[pstok], writes=[pttok], scale=0.125)
                    for jj, i in enumerate(grp):
                        kc = keychunks[i][0]
                        S.mm(pn[:, 0:n], vfn(kc), pt[:, jj, 0:n], i == 0, i == nk - 1, reads=[rtok, pttok], writes=[pntok])
                rd, rdtok = rdr.next()
                S.act(rd[:, 0:n], pn[64:128, 0:n], AF.Copy, reads=[pntok], writes=[rdtok])
                S.recip(rd[:, 0:n], rd[:, 0:n], reads=[rdtok], writes=[rdtok])
                ob, obtok = obr.next()
                S.tt("dve", ob[:, 0:n], pn[0:64, 0:n], rd[:, 0:n], ALU.mult, reads=[pntok, rdtok], writes=[obtok])
                S.dma("pool" if self._dq % 2 == 0 else "sp", dst[h * 64:(h + 1) * 64, tq0:tq0 + n], ob[:, 0:n], reads=[obtok])
                self._dq += 1
                S.maybe_flush()

            ctxk = [(0, None, None), (1, None, None)]
            for h in range(6):
                hb = (h % 2) * 64
                vn_fn = lambda kc, h=h: vn[:, kc, h, :]
                cur = None
                for B in range(NB):
                    kw = min(max(8 * B - 4, 0), NR - 16)
                    case = {0: 0, -4: 1, -8: 2}[kw - 8 * B]
                    if case != cur:
                        bt, bttok = btr.next()
                        S.dma("sp", bt[:], self.BT[l, h, case].rearrange("c p q -> p c q"), writes=[bttok])
                        cur = case
                    keys = [(2 + (kw + 2 * c) // 2, bt[:, c, :], bttok) for c in range(8)] + ctxk
                    attend(self.NAQT, h, hb, L_CTX + 512 * B, 512, kt_n, h // 2, keys, vn_fn, self.ONAT)
                if ctx_out:
                    attend(self.NAQT, h, hb, 0, L_CTX, kt_n, h // 2, ctxk, vn_fn, self.ONAT)
            allk = [(kc, None, None) for kc in range(NCH)]
            for h in range(6):
                hb = (h % 2) * 64
                kvh = h // 3
                vg_fn = lambda kc, kvh=kvh: vg[:, kc, kvh, :]
                for B in range(NB):
                    attend(self.GQQT, h, hb, L_CTX + 512 * B, 512, kt_g, kvh, allk, vg_fn, self.OGQT)
                if ctx_out:
                    attend(self.GQQT, h, hb, 0, L_CTX, kt_g, kvh, ctxk, vg_fn, self.OGQT)
            S.flush()

    def phase_merge(self, l, ctx_out):
        S = self.S
        with ExitStack() as es:
            self.es = es
            wml = self.sb("g_wml", [128, 2, D], BF16)
            wna = self.sb("g_wna", [128, 3, D], BF16)
            wgq = self.sb("g_wgq", [128, 3, D], BF16)
            wo = self.sb("g_wo", [128, 8, D], BF16)
            self.wtok = S.tok()
            wtok = self.wtok
            stage = Ring(self, "g_st", 4, [128, 1024], F32)
            self.load_w(wml, self.w_br_ml[l], 2, D, stage, 1024)
            self.load_w(wna, self.w_br_na[l], 3, D, stage, 1024)
            self.load_w(wgq, self.w_br_gq[l], 3, D, stage, 1024)
            self.load_w(wo, self.w_out[l], 8, D, stage, 1024)
            hfr = Ring(self, "g_hf", 2, [128, 3, 2, 512], F32)
            onr = Ring(self, "g_on", 2, [128, 2, 3, 512], BF16)
            xr = Ring(self, "g_x", 2, [128, 8, 512], F32)
            hsr = Ring(self, "g_hs", 2, [128, 512], F32)
            sqr = Ring(self, "g_sq", 2, [128, 512], F32)
            omr = Ring(self, "g_om", 1, [128, 2, 512], BF16)
            ggr = Ring(self, "g_gg", 8, [128, 3, 512], F32)
            t1r = Ring(self, "g_t1", 2, [128, 512], F32)
            t2r = Ring(self, "g_t2", 2, [128, 512], F32)
            yr = Ring(self, "g_y", 1, [128, 8, 512], BF16)
            pst = Ring(self, "g_pst", 1, [128, 512], F32, psum=True)
            p1r = Ring(self, "g_p1", 2, [128, 512], F32, psum=True)
            p2r = Ring(self, "g_p2", 2, [128, 512], F32, psum=True)
            p3r = Ring(self, "g_p3", 2, [128, 512], F32, psum=True)
            por = Ring(self, "g_po", 1, [128, 512], F32, psum=True)
            tl = [t for t in self.tiles(512) if not (t[2] and not ctx_out)]
            brg4 = lambda t0, n: self.BRGT[:, t0:t0 + n].rearrange("(b o p) t -> p b o t", b=3, p=128)

            def mload(ti):
                t0, n, isctx = tl[ti]
                hf, hftok = hfr.next()
                for i, src in enumerate((self.HFT, self.HBT, self.SIGOT)):
                    S.dma("sp", hf[:, i, :, 0:n], src[:, t0:t0 + n].rearrange("(c p) t -> p c t", p=128), writes=[hftok])
                on, ontok = onr.next()
                for i, src in enumerate((self.ONAT, self.OGQT)):
                    S.dma("pool", on[:, i, :, 0:n], src[:, t0:t0 + n].rearrange("(c p) t -> p c t", p=128), writes=[ontok])
                xt, xtok = xr.next()
                S.dma("sp", xt[:, :, 0:n], self.XT[:, t0:t0 + n].rearrange("(kc p) t -> p kc t", p=128), writes=[xtok])
                return hf, hftok, on, ontok, xt, xtok

            def gload(ti, oc):
                t0, n, isctx = tl[ti]
                gg, ggtok = ggr.next()
                S.dma(("pool", "act", "sp")[oc % 3], gg[:, :, 0:n], brg4(t0, n)[:, :, oc, :], writes=[ggtok])
                return gg, ggtok

            nxt = mload(0)
            gq = [gload(0, oc) for oc in range(6)]
            for ti, (t0, n, isctx) in enumerate(tl):
                c = 1 if isctx else 0
                hf, hftok, on, ontok, xt, xtok = nxt
                om, omtok = omr.next()
                for cc in range(2):
                    hs, hstok = hsr.next()
                    S.tt("dve", hs[:, 0:n], hf[:, 0, cc, 0:n], hf[:, 1, cc, 0:n], ALU.add, reads=[hftok], writes=[hstok])
                    sq, sqtok = sqr.next()
                    S.act(sq[:, 0:n], hs[:, 0:n], AF.Square, reads=[hstok], writes=[sqtok])
                    ps_, pstok = pst.next()
                    S.mm(ps_[:, 0:n], self.c_ones64, sq[:, 0:n], True, True, reads=[sqtok], writes=[pstok])
                    S.act(sq[:, 0:n], ps_[:, 0:n], AF.Sqrt, reads=[pstok], writes=[sqtok], bias=self.c_eps, scale=1.0)
                    S.recip(sq[:, 0:n], sq[:, 0:n], reads=[sqtok], writes=[sqtok])
                    S.stt("dve", hs[:, 0:n], hs[:, 0:n], self.vec_mlnw[:, l, cc:cc + 1], sq[:, 0:n], ALU.mult, ALU.mult, reads=[hstok, sqtok], writes=[hstok])
                    S.tt("dve", om[:, cc, 0:n], hs[:, 0:n], hf[:, 2, cc, 0:n], ALU.mult, reads=[hstok, hftok], writes=[omtok])
                y, ytok = yr.next()
                for oc in range(8):
                    gg, ggtok = gq.pop(0)
                    nti, noc = (ti, oc + 6) if oc + 6 < 8 else (ti + 1, oc + 6 - 8)
                    if nti < len(tl):
                        gq.append(gload(nti, noc))
                    if oc == 2 and ti + 1 < len(tl):
                        nxt = mload(ti + 1)
                    p1, p1tok = p1r.next()
                    p2, p2tok = p2r.next()
                    p3, p3tok = p3r.next()
                    osl = slice(oc * 128, (oc + 1) * 128)
                    for cc in range(2):
                        S.mm(p1[:, 0:n], wml[:, cc, osl], om[:, cc, 0:n], cc == 0, cc == 1, reads=[wtok, omtok], writes=[p1tok])
                    for cc in range(3):
                        S.mm(p2[:, 0:n], wna[:, cc, osl], on[:, 0, cc, 0:n], cc == 0, cc == 2, reads=[wtok, ontok], writes=[p2tok])
                    for cc in range(3):
                        S.mm(p3[:, 0:n], wgq[:, cc, osl], on[:, 1, cc, 0:n], cc == 0, cc == 2, reads=[wtok, ontok], writes=[p3tok])
                    t1, t1tok = t1r.next()
                    t2, t2tok = t2r.next()
                    S.tt("dve", t1[:, 0:n], p1[:, 0:n], gg[:, 0, 0:n], ALU.mult, reads=[p1tok, ggtok], writes=[t1tok])
                    S.tt("dve", t2[:, 0:n], p2[:, 0:n], gg[:, 1, 0:n], ALU.mult, reads=[p2tok, ggtok], writes=[t2tok])
                    S.tt("pool", t1[:, 0:n], t1[:, 0:n], t2[:, 0:n], ALU.add, reads=[t1tok, t2tok], writes=[t1tok])
                    S.tt("dve", t2[:, 0:n], p3[:, 0:n], gg[:, 2, 0:n], ALU.mult, reads=[p3tok, ggtok, t1tok], writes=[t2tok])
                    S.tt("pool", y[:, oc, 0:n], t1[:, 0:n], t2[:, 0:n], ALU.add, reads=[t1tok, t2tok], writes=[ytok])
                for oc in range(8):
                    po, potok = por.next()
                    for kc in range(8):
                        S.mm(po[:, 0:n], wo[:, kc, oc * 128:(oc + 1) * 128], y[:, kc, 0:n], kc == 0, kc == 7, reads=[wtok, ytok], writes=[potok])
                    S.stt("dve", xt[:, oc, 0:n], po[:, 0:n], self.modG[:, 1, c, oc:oc + 1], xt[:, oc, 0:n], ALU.mult, ALU.add,
                          reads=[potok, self.modtok, xtok], writes=[xtok])
                S.dma("pool", self.XT[:, t0:t0 + n].rearrange("(kc p) t -> p kc t", p=128), xt[:, :, 0:n], reads=[xtok])
                S.maybe_flush()
            S.flush()

    def phase_setup(self):
        S = self.S
        with ExitStack() as es:
            self.es = es
            rm = self.sb("b_rm", [128, 3, 8, 512], F32)
            rmtok = S.tok()
            S.dma("sp", rm[:], self.rowmask_d.rearrange("c k p q -> p c k q"), writes=[rmtok])
            tr = Ring(self, "b_t", 4, [128, 8, 64], F32)
            stage = Ring(self, "adst", 2, [128, 8, 1024], F32)
            sc = self.sb("ad_sc", [128, 8, 2])
            sctok = S.tok()
            pst = self.ps("ad_ps", [128, 8, 2])
            pstok = S.tok()
            S.act(sc[:], self.vec_c[:], AF.Silu, writes=[sctok])

            def bias_gen():
                i = 0
                for l in range(self.layers):
                    for h in range(6):
                        for case, delta in ((0, 0), (1, -4), (2, -8)):
                            for c in range(8):
                                t, ttok = tr.next()
                                for a in range(2):
                                    r0 = 15 - 2 * c - a - delta
                                    S.dma("sp" if i % 2 == 0 else "pool", t[a * 64:(a + 1) * 64, :, :],
                                          self.rpbT_d[l, h, r0:r0 + 8].rearrange("j k q -> k j q"), writes=[ttok])
                                    i += 1
                                tv = t[:].rearrange("p j q -> p (j q)")
                                S.tt("pool", tv, tv, rm[:, case, c, :], ALU.add, reads=[ttok, rmtok], writes=[ttok])
                                S.act(tv, tv, AF.Identity, reads=[ttok], writes=[ttok], scale=8.0)
                                S.dma("act", self.BT[l, h, case, c], tv, reads=[ttok])
                                yield

            bg = bias_gen()

            def bias_steps(k):
                for _ in range(k):
                    try:
                        next(bg)
                    except StopIteration:
                        return

            for l in range(self.layers):
                modraw, modA, modG = self.modraw_all[:, l], self.modA_all[:, l], self.modG_all[:, l]
                awv = self.ada_w[l].rearrange("(kc p) n -> p kc n", p=128)
                for i in range(9):
                    st, stok = stage.next()
                    S.dma("sp", st[:, 0:4, :], awv[:, 0:4, i * 1024:(i + 1) * 1024], writes=[stok])
                    S.dma("pool", st[:, 4:8, :], awv[:, 4:8, i * 1024:(i + 1) * 1024], writes=[stok])
                    for oc in range(8):
                        for kc in range(8):
                            S.mm(pst[:, oc, :], st[:, kc, oc * 128:(oc + 1) * 128], sc[:, kc, :], kc == 0, kc == 7,
                                 reads=[stok, sctok], writes=[pstok])
                    for c in range(2):
                        S.tt("dve", modraw[:, i, :, c], pst[:, :, c], self.vec_adab[:, l, i, :], ALU.add,
                             reads=[pstok], writes=[self.modtok])
                    bias_steps(16)
                for j in range(3):
                    for c in range(2):
                        S.stt("dve", modA[:, j, c, :], modraw[:, 3 * j + 1, :, c], 1.0, self.vec_normw[:, l, j, :],
                              ALU.add, ALU.mult, reads=[self.modtok], writes=[self.modtok])
                        S.ts("dve", modG[:, j, c, :], modraw[:, 3 * j + 2, :, c], 0.5 if j != 1 else 1.0, None, ALU.mult,
                             reads=[self.modtok], writes=[self.modtok])
            bias_steps(10 ** 6)
            S.flush()

    def build(self):
        nc = self.nc
        T, S_lat, LY = self.T, self.S_lat, self.layers
        NV = 256
        self.xin = self.din("xin", [D, T])
        self.ada_w = self.din("ada_w", [2, D, 9 * D])
        self.ffn_w_in = self.din("ffn_w_in", [2, 2, D, 2 * DFF])
        self.ffn_w_out = self.din("ffn_w_out", [2, 2, DFF, D])
        self.mix_w_in = self.din("mix_w_in", [2, D, NIN])
        self.w_br_ml = self.din("w_br_ml", [2, 256, D])
        self.w_br_na = self.din("w_br_na", [2, 384, D])
        self.w_br_gq = self.din("w_br_gq", [2, 384, D])
        self.w_out = self.din("w_out", [2, D, D])
        self.consts_d = self.din("consts", [128, 1280])
        self.vecs_d = self.din("vecs", [128, NV])
        self.rope_d = self.din("rope", [2, 128, S_lat])
        self.rowmask_d = self.din("rowmask", [3, 8, 128, 512])
        self.rpbT_d = self.din("rpbT", [2, 6, 31, 64, 64])
        self.out = self.nc.dram_tensor("out", [D, S_lat], F32, kind="ExternalOutput").ap()
        self.XT = self.dscr("XT", [D, T])
        self.MLQT = self.dscr("MLQT", [256, T])
        self.MLKT = self.dscr("MLKT", [256, T])
        self.MLKV = self.dscr("MLKV", [T, 512])
        self.MLG = self.dscr("MLG", [T, 16])
        self.SIGOT = self.dscr("SIGOT", [256, T])
        self.NAQT = self.dscr("NAQT", [384, T], BF16)
        self.NAKT = self.dscr("NAKT", [384, T], BF16)
        self.NAV = self.dscr("NAV", [T, 384], BF16)
        self.GQQT = self.dscr("GQQT", [384, T], BF16)
        self.GQKT = self.dscr("GQKT", [128, T], BF16)
        self.GQV = self.dscr("GQV", [T, 128], BF16)
        self.BRGT = self.dscr("BRGT", [3 * D, T])
        self.HFT = self.dscr("HFT", [256, T])
        self.HBT = self.dscr("HBT", [256, T])
        self.ONAT = self.dscr("ONAT", [384, T], BF16)
        self.OGQT = self.dscr("OGQT", [384, T], BF16)
        self.BT = self.nc.dram_tensor("BT", [2, 6, 3, 8, 128, 512], F32).ap()
        with ExitStack() as pes:
            self.S = Sched(nc, pes)
            S = self.S
            self.cst = pes.enter_context(nc.sbuf_tensor("cst_sb", [128, 1280], F32))
            self.vecs = pes.enter_context(nc.sbuf_tensor("vecs_sb", [128, NV], F32))
            self.modraw_all = pes.enter_context(nc.sbuf_tensor("modraw", [128, 2, 9, 8, 2], F32))
            self.modA_all = pes.enter_context(nc.sbuf_tensor("modA", [128, 2, 3, 2, 8], F32))
            self.modG_all = pes.enter_context(nc.sbuf_tensor("modG", [128, 2, 3, 2, 8], F32))
            self.c_onesD = self.cst[:, 0:128]
            self.c_ones64 = self.cst[:, 128:256]
            self.c_rot = self.cst[:, 256:384]
            self.c_eps = self.cst[:, 1024:1025]
            self.c_one = self.cst[:, 1025:1026]
            self.vec_c = self.vecs[:, 0:16].rearrange("p (k c) -> p k c", c=2)
            self.vec_normw = self.vecs[:, 16:64].rearrange("p (l j k) -> p l j k", l=2, j=3)
            self.vec_adab = self.vecs[:, 64:208].rearrange("p (l i k) -> p l i k", l=2, i=9)
            self.vec_mlnw = self.vecs[:, 208:212].rearrange("p (l c) -> p l c", l=2)
            self.vec_qkw = self.vecs[:, 212:220].rearrange("p (l c) -> p l c", l=2)
            self.vec_gateb = self.vecs[:, 220:252].rearrange("p (l c) -> p l c", l=2)
            self.modtok = S.tok()
            t0 = S.tok()
            S.dma("sp", self.cst[:], self.consts_d, writes=[t0])
            S.dma("sp", self.vecs[:], self.vecs_d, writes=[t0])
            S.flush()
            self.run_phases()
        return nc

    def run_phases(self):
        LY = self.layers
        self.phase_setup()
        if self.done("biasprep"):
            return
        for l in range(LY):
            lastl = l == LY - 1
            self.modraw, self.modA, self.modG = self.modraw_all[:, l], self.modA_all[:, l], self.modG_all[:, l]
            self.phase_ffn(l, 0, self.xin if l == 0 else self.XT, self.XT)
            if self.done(f"ffn{l}0"):
                return
            self.phase_inproj(l)
            if self.done(f"inproj{l}"):
                return
            self.phase_mixers(l, not lastl)
            if self.done(f"attn{l}"):
                return
            self.phase_merge(l, not lastl)
            if self.done(f"merge{l}"):
                return
            self.phase_ffn(l, 1, self.XT, self.out if lastl else self.XT, last=lastl)


def host_consts():
    c = np.zeros((128, 1280), np.float32)
    c[:, 0:128] = 1.0 / D
    for b in range(2):
        c[b * 64:(b + 1) * 64, 128 + b * 64:128 + (b + 1) * 64] = 1.0 / 64
    for b in range(2):
        for m in range(64):
            if m < 32:
                c[b * 64 + m + 32, 256 + b * 64 + m] = -1.0
            else:
                c[b * 64 + m - 32, 256 + b * 64 + m] = 1.0
    c[:, 384:512] = np.eye(128, dtype=np.float32)
    li = np.arange(128)[:, None]
    ji = np.arange(128)[None, :]
    c[:, 512:640] = (li <= ji)
    c[:, 640:768] = (li >= ji)
    c[:, 768:896] = -c[:, 512:640]
    c[:, 896:1024] = -c[:, 640:768]
    c[:, 1024] = EPS
    c[:, 1025] = 1.0
    c[:, 1088:1152] = 1.0
    return c


def host_vecs(c_b, c_ctx, norm_w, ada_b, ml_norm_w, na_qk_w, gq_qk_w, ml_gate_b):
    v = np.zeros((128, 256), np.float32)
    cc = np.stack([c_b.reshape(8, 128).T, c_ctx.reshape(8, 128).T], axis=-1)
    v[:, 0:16] = cc.reshape(128, 16)
    v[:, 16:64] = norm_w.reshape(2, 3, 8, 128).transpose(3, 0, 1, 2).reshape(128, 48)
    v[:, 64:208] = ada_b.reshape(2, 9, 8, 128).transpose(3, 0, 1, 2).reshape(128, 144)
    v[:, 208:212] = ml_norm_w.reshape(2, 2, 128).transpose(2, 0, 1).reshape(128, 4)
    qk = np.stack([na_qk_w[:, 0], na_qk_w[:, 1], gq_qk_w[:, 0], gq_qk_w[:, 1]], axis=1)
    v[:, 212:220] = np.tile(qk, (1, 1, 2)).transpose(2, 0, 1).reshape(128, 8)
    v[:, 220:252] = np.broadcast_to(ml_gate_b.reshape(1, 32), (128, 32))
    return v


def host_rope(S):
    t = np.arange(S)
    row = (t // 64).astype(np.float32)
    col = (t % 64).astype(np.float32)
    nf = 16
    inv = (np.float32(10000.0) ** (-np.arange(nf, dtype=np.float32) / nf)).astype(np.float32)
    ang = np.concatenate([row[:, None] * inv, col[:, None] * inv], axis=-1).astype(np.float32)
    idx = np.arange(128) % 32
    r = np.zeros((2, 128, S), np.float32)
    r[0] = np.cos(ang)[:, idx].T
    r[1] = np.sin(ang)[:, idx].T
    return r


def host_rowmask(NR):
    NB = NR // 8
    m = np.zeros((3, 8, 128, 512), np.float32)
    for case, B in ((0, 0), (1, min(1, NB - 1)), (2, NB - 1)):
        kw = min(max(8 * B - 4, 0), NR - 16)
        for c in range(8):
            for a in range(2):
                kr = kw + 2 * c + a
                for j in range(8):
                    qr = 8 * B + j
                    rs = min(max(qr - 4, 0), NR - 8)
                    if not (rs <= kr < rs + 8):
                        m[case, c, a * 64:(a + 1) * 64, j * 64:(j + 1) * 64] = NEG
    return m


def host_rpbT(na_rpb):
    L = na_rpb.shape[0]
    kc = np.arange(64)[:, None]
    qc = np.arange(64)[None, :]
    cs = np.clip(qc - 8, 0, 48)
    valid = (kc >= cs) & (kc < cs + 16)
    off = np.clip(kc - qc + 15, 0, 30)
    out = np.full((L, 6, 31, 64, 64), NEG, np.float32)
    for r in range(31):
        ro = 22 - r
        if 0 <= ro <= 14:
            g = na_rpb[:, :, ro, :][:, :, off]
            out[:, :, r] = np.where(valid[None, None], g, np.float32(NEG))
    return out


def make_in_maps(x, c, ctx, c_ctx, ada_w, ada_b, norm_w, ffn_w_in, ffn_w_out, mix_w_in, ml_gate_b, ml_norm_w, na_qk_w,
                 na_rpb, gq_qk_w, w_br_ml, w_br_na, w_br_gq, w_out):
    f = lambda a: np.ascontiguousarray(np.asarray(a, dtype=np.float32))
    x, c, ctx, c_ctx = f(x), f(c), f(ctx), f(c_ctx)
    B, S, _ = x.shape
    consts = host_consts()
    rope = host_rope(S)
    rowmask = host_rowmask(S // 64)
    rpbT = host_rpbT(f(na_rpb))
    shared = {"ada_w": f(ada_w), "ffn_w_in": f(ffn_w_in), "ffn_w_out": f(ffn_w_out), "mix_w_in": f(mix_w_in),
              "w_br_ml": f(w_br_ml), "w_br_na": f(w_br_na), "w_br_gq": f(w_br_gq), "w_out": f(w_out),
              "consts": consts, "rope": rope, "rowmask": rowmask, "rpbT": rpbT}
    in_maps = []
    for b in range(B):
        m = dict(shared)
        m["xin"] = np.ascontiguousarray(np.concatenate([ctx[b].T, x[b].T], axis=1))
        m["vecs"] = host_vecs(c[b], c_ctx, f(norm_w), f(ada_b), f(ml_norm_w), f(na_qk_w), f(gq_qk_w), f(ml_gate_b))
        in_maps.append(m)
    return in_maps


def kernel(**inputs):
    x = np.asarray(inputs["x"])
    B, S, _ = x.shape
    kb = K(S=S)
    nc = kb.build()
    in_maps = make_in_maps(**inputs)
    res = run_bass_kernel_spmd(nc, in_maps, core_ids=list(range(B)))
    out = np.stack([np.ascontiguousarray(r["out"].T) for r in res.results], axis=0)
    return out.astype(np.float32)
```
